# Optimizing a Trainium2 kernel written in Bass

```python
import math
import jax
import jax.numpy as jnp
from jax import lax
import numpy as np

D_MODEL = 1024
BATCH = 32
SEQ = 256
DEPTH = 2
DEC_BATCH = 8
DEC_SEQ = 2048
PAST_LEN = 256

GRID_W = 64
ROPE_THETA = 10000.0
EPS = 1e-6
Q_BLOCK = 128
MLA_HEADS = 4
MLA_NOPE = 64
MLA_ROPE = 32
MLA_V = 64
MLA_Q_RANK = 256
MLA_KV_RANK = 128
ML_HEADS = 4
ML_DH = 128
ML_WIDTH = ML_HEADS * ML_DH
ML_CHUNK = 128
GQA_HEADS = 4
GQA_KV_HEADS = 2
GQA_DH = 64
D_FF = 2816

A_WIDTH = MLA_HEADS * MLA_V
C_WIDTH = GQA_HEADS * GQA_DH
MIX_WIDTH = A_WIDTH + ML_WIDTH + C_WIDTH
IN_SIZES = (MLA_Q_RANK, MLA_KV_RANK, MLA_ROPE, ML_WIDTH, ML_WIDTH, ML_WIDTH, 4 * ML_HEADS,
            GQA_HEADS * GQA_DH, GQA_KV_HEADS * GQA_DH, GQA_KV_HEADS * GQA_DH)
IN_COLS = MLA_Q_RANK + MLA_KV_RANK + MLA_ROPE + 3 * ML_WIDTH + 4 * ML_HEADS + (GQA_HEADS + 2 * GQA_KV_HEADS) * GQA_DH
MLA_SCALE = (MLA_NOPE + MLA_ROPE) ** -0.5
GQA_SCALE = GQA_DH ** -0.5

kernel_name = 'hybrid_mla_mlstm_gqa_diffusion_step'


def rmsnorm(x, g):
    xf = x.astype(jnp.float32)
    xf = xf * lax.rsqrt(jnp.mean(xf * xf, axis=-1, keepdims=True) + EPS)
    return xf.astype(x.dtype) * g


def grid_positions(n_tokens):
    rows = n_tokens // GRID_W
    r, c = jnp.meshgrid(jnp.arange(rows, dtype=jnp.int32), jnp.arange(GRID_W, dtype=jnp.int32), indexing='ij')
    return r.reshape(-1), c.reshape(-1)


def rope_1d(x, pos):
    d = x.shape[-1]
    inv = ROPE_THETA ** (-jnp.arange(0, d, 2, dtype=jnp.float32) / d)
    ang = pos.astype(jnp.float32)[:, None] * inv[None, :]
    cos = jnp.cos(ang)[:, None, :].astype(x.dtype)
    sin = jnp.sin(ang)[:, None, :].astype(x.dtype)
    x1, x2 = x[..., : d // 2], x[..., d // 2:]
    return jnp.concatenate([x1 * cos - x2 * sin, x1 * sin + x2 * cos], axis=-1)


def rope_2d(x, rows, cols):
    h = x.shape[-1] // 2
    return jnp.concatenate([rope_1d(x[..., :h], rows), rope_1d(x[..., h:], cols)], axis=-1)


def dwconv_centred(x, w, b):
    k = w.shape[0]
    p = k // 2
    t = x.shape[1]
    xp = jnp.pad(x, ((0, 0), (p, p), (0, 0)))
    return sum(xp[:, j:j + t] * w[j] for j in range(k)) + b


def modulation(cvec, lp):
    m = jax.nn.silu(cvec) @ lp['w_ada'] + lp['b_ada']
    return jnp.split(m[:, None, :], 6, axis=-1)


def adaln_in(x, g, shift, scale):
    return rmsnorm(x, g) * (1.0 + scale) + shift


def split_in(z):
    idx = [int(v) for v in np.cumsum(IN_SIZES)[:-1]]
    return jnp.split(z, idx, axis=-1)


def block_attention(q, k, v, scale):
    b, tq, h, dk = q.shape
    hk = k.shape[2]
    g = h // hk
    nb = tq // Q_BLOCK
    qb = q.reshape(b, nb, Q_BLOCK, hk, g, dk).transpose(1, 0, 2, 3, 4, 5)

    def one(qblk):
        s = jnp.einsum('bqhgd,bkhd->bhgqk', qblk, k).astype(jnp.float32) * scale
        p = jax.nn.softmax(s, axis=-1).astype(v.dtype)
        return jnp.einsum('bhgqk,bkhd->bqhgd', p, v)

    o = lax.map(one, qb)
    return o.transpose(1, 0, 2, 3, 4, 5).reshape(b, tq, h, v.shape[-1])


def mla_queries(cq, lp):
    b, t, _ = cq.shape
    q = (rmsnorm(cq, lp['g_mla_q']) @ lp['w_mla_uq']).reshape(b, t, MLA_HEADS, MLA_NOPE + MLA_ROPE)
    return q[..., :MLA_NOPE], q[..., MLA_NOPE:]


def mla_kv(ckv, lp):
    b, t, _ = ckv.shape
    kv = (ckv @ lp['w_mla_ukv']).reshape(b, t, MLA_HEADS, MLA_NOPE + MLA_V)
    return kv[..., :MLA_NOPE], kv[..., MLA_NOPE:]


def mla_keys(k_nope, k_rope):
    kr = jnp.broadcast_to(k_rope[:, :, None, :], k_nope.shape[:-1] + (MLA_ROPE,))
    return jnp.concatenate([k_nope, kr], axis=-1)


def gqa_qkv(q_g, k_g, v_g, lp):
    b, t, _ = q_g.shape
    q = rmsnorm(q_g.reshape(b, t, GQA_HEADS, GQA_DH), lp['g_gqa_q'])
    k = rmsnorm(k_g.reshape(b, t, GQA_KV_HEADS, GQA_DH), lp['g_gqa_k'])
    v = v_g.reshape(b, t, GQA_KV_HEADS, GQA_DH)
    return q, k, v


def mlstm_chunkwise(q, k, v, i_pre, f_pre, c0, n0, m0):
    b, t, h, d = q.shape
    lc = ML_CHUNK
    nc = t // lc

    def to_chunks(a):
        a = a.astype(jnp.float32).reshape((b, nc, lc, h) + a.shape[3:])
        return jnp.swapaxes(jnp.moveaxis(a, 1, 0), 2, 3)

    li_all = i_pre
    lf_all = jax.nn.log_sigmoid(f_pre.astype(jnp.float32))
    xs = (to_chunks(q), to_chunks(k), to_chunks(v), to_chunks(li_all), to_chunks(lf_all))
    tri = jnp.tril(jnp.ones((lc, lc), dtype=bool))

    def step(carry, xc):
        cm, nm, mm = carry
        qc, kc, vc, li, lf = xc
        bc = jnp.cumsum(lf, axis=-1)
        dlog = bc[..., :, None] - bc[..., None, :] + li[..., None, :]
        dlog = jnp.where(tri, dlog, -jnp.inf)
        g = bc + mm[..., None]
        m_t = jnp.maximum(g, jnp.max(dlog, axis=-1))
        w = jnp.exp(dlog - m_t[..., None])
        inter = jnp.exp(g - m_t)
        s = jnp.einsum('bhtd,bhsd->bhts', qc, kc) * w
        num = jnp.einsum('bhts,bhsd->bhtd', s, vc) + inter[..., None] * jnp.einsum('bhtd,bhde->bhte', qc, cm)
        den = jnp.sum(s, axis=-1) + inter * jnp.einsum('bhtd,bhd->bht', qc, nm)
        hc = num / jnp.maximum(jnp.abs(den), jnp.exp(-m_t))[..., None]
        b_last = bc[..., -1]
        m_new = m_t[..., -1]
        wk = jnp.exp(b_last[..., None] - bc + li - m_new[..., None])
        decay = jnp.exp(b_last + mm - m_new)
        c_new = decay[..., None, None] * cm + jnp.einsum('bhs,bhsd,bhse->bhde', wk, kc, vc)
        n_new = decay[..., None] * nm + jnp.einsum('bhs,bhsd->bhd', wk, kc)
        return (c_new, n_new, m_new), hc

    init = (c0.astype(jnp.float32), n0.astype(jnp.float32), m0.astype(jnp.float32))
    (cf, nf, mf), hs = lax.scan(step, init, xs)
    hs = hs.transpose(1, 0, 3, 2, 4).reshape(b, t, h, d)
    return hs, (cf, nf, mf)


def mlstm_mixer(u, v_ml, o_ml, gates, lp, c0, n0, m0):
    b, t, _ = u.shape
    uc = jax.nn.silu(dwconv_centred(u, lp['w_ml_conv'], lp['b_ml_conv'])).reshape(b, t, ML_HEADS, ML_DH)
    q = jnp.einsum('bthd,hde->bthe', uc, lp['w_ml_q'])
    k = jnp.einsum('bthd,hde->bthe', uc, lp['w_ml_k']) * (ML_DH ** -0.5)
    v = v_ml.reshape(b, t, ML_HEADS, ML_DH)
    g = (gates + lp['b_ml_gates']).astype(jnp.float32).reshape(b, t, 4, ML_HEADS)
    h_f, (cf, nf, mf) = mlstm_chunkwise(q, k, v, g[:, :, 0], g[:, :, 1], c0[:, 0], n0[:, 0], m0[:, 0])
    rev = lambda a: jnp.flip(a, axis=1)
    h_b, (cb, nb, mb) = mlstm_chunkwise(rev(q), rev(k), rev(v), rev(g[:, :, 2]), rev(g[:, :, 3]),
                                        c0[:, 1], n0[:, 1], m0[:, 1])
    hsum = (h_f + rev(h_b)).astype(u.dtype)
    hn = rmsnorm(hsum, lp['g_ml_out'].reshape(ML_HEADS, ML_DH)).reshape(b, t, ML_WIDTH)
    out = hn * jax.nn.sigmoid(o_ml)
    return out, (jnp.stack([cf, cb], axis=1), jnp.stack([nf, nb], axis=1), jnp.stack([mf, mb], axis=1))


def merge_heads(o_a, o_b, o_c, lp):
    b, t = o_b.shape[:2]
    o = jnp.concatenate([o_a.reshape(b, t, A_WIDTH), o_b, o_c.reshape(b, t, C_WIDTH)], axis=-1)
    return o @ lp['w_out']


def mixer_context(y, lp):
    b, t, _ = y.shape
    cq, ckv_raw, krope, u, v_ml, o_ml, gates, q_g, k_g, v_g = split_in(y @ lp['w_in'])
    ckv = rmsnorm(ckv_raw, lp['g_mla_kv'])
    q_nope, q_rope = mla_queries(cq, lp)
    k_nope, v_a = mla_kv(ckv, lp)
    o_a = block_attention(jnp.concatenate([q_nope, q_rope], axis=-1), mla_keys(k_nope, krope), v_a, MLA_SCALE)
    c0 = jnp.zeros((b, 2, ML_HEADS, ML_DH, ML_DH), y.dtype)
    n0 = jnp.zeros((b, 2, ML_HEADS, ML_DH), y.dtype)
    m0 = jnp.zeros((b, 2, ML_HEADS), y.dtype)
    o_b, (cs, ns, ms) = mlstm_mixer(u, v_ml, o_ml, gates, lp, c0, n0, m0)
    q_c, k_c, v_c = gqa_qkv(q_g, k_g, v_g, lp)
    o_c = block_attention(q_c, k_c, v_c, GQA_SCALE)
    return merge_heads(o_a, o_b, o_c, lp), (ckv, krope, k_c, v_c, cs, ns, ms)


def mixer_latent(y, lp, ctx, rows, cols):
    ckv_ctx, krope_ctx, k_ctx, v_ctx, c0, n0, m0 = ctx
    cq, ckv_raw, krope, u, v_ml, o_ml, gates, q_g, k_g, v_g = split_in(y @ lp['w_in'])
    q_nope, q_rope = mla_queries(cq, lp)
    q_a = jnp.concatenate([q_nope, rope_2d(q_rope, rows, cols)], axis=-1)
    k_nope, v_lat = mla_kv(rmsnorm(ckv_raw, lp['g_mla_kv']), lp)
    kr = rope_2d(krope[:, :, None, :], rows, cols)[:, :, 0]
    k_nope_c, v_ctx_a = mla_kv(ckv_ctx, lp)
    k_a = jnp.concatenate([mla_keys(k_nope, kr), mla_keys(k_nope_c, krope_ctx)], axis=1)
    v_a = jnp.concatenate([v_lat, v_ctx_a], axis=1)
    o_a = block_attention(q_a, k_a, v_a, MLA_SCALE)
    o_b, _ = mlstm_mixer(u, v_ml, o_ml, gates, lp, c0, n0, m0)
    q_c, k_c, v_c = gqa_qkv(q_g, k_g, v_g, lp)
    q_c = rope_2d(q_c, rows, cols)
    k_c = rope_2d(k_c, rows, cols)
    o_c = block_attention(q_c, jnp.concatenate([k_c, k_ctx], axis=1), jnp.concatenate([v_c, v_ctx], axis=1), GQA_SCALE)
    return merge_heads(o_a, o_b, o_c, lp)


def conv_ffn(y, lp):
    a, g = jnp.split(y @ lp['w_ff_up'], 2, axis=-1)
    g = dwconv_centred(g, lp['w_ff_conv'], lp['b_ff_conv'])
    return (jax.nn.silu(g) * a) @ lp['w_ff_down']


def setup_inputs(seed: int = 0) -> dict:
    key = jax.random.key(seed)
    ks = iter(jax.random.split(key, 40))

    def nrm(shape, scale=1.0):
        return jax.random.normal(next(ks), shape, jnp.float32) * scale

    L = DEPTH
    gate_base = jnp.repeat(jnp.array([0.0, 3.0, 0.0, 3.0], jnp.float32), ML_HEADS)
    return {
        'x_prompt': nrm((BATCH, SEQ, D_MODEL)),
        'x_sample': nrm((DEC_BATCH, DEC_SEQ, D_MODEL)),
        'cache_mla_ckv': nrm((DEC_BATCH, L, PAST_LEN, MLA_KV_RANK)),
        'cache_mla_krope': nrm((DEC_BATCH, L, PAST_LEN, MLA_ROPE)),
        'cache_gqa_k': nrm((DEC_BATCH, L, PAST_LEN, GQA_KV_HEADS, GQA_DH)),
        'cache_gqa_v': nrm((DEC_BATCH, L, PAST_LEN, GQA_KV_HEADS, GQA_DH)),
        'state_mlstm_C': nrm((DEC_BATCH, L, 2, ML_HEADS, ML_DH, ML_DH), 0.1),
        'state_mlstm_n': nrm((DEC_BATCH, L, 2, ML_HEADS, ML_DH), 0.1),
        'state_mlstm_m': nrm((DEC_BATCH, L, 2, ML_HEADS)),
        'c': nrm((DEC_BATCH, D_MODEL)),
        'c_ctx': nrm((D_MODEL,)),
        'w_ada': nrm((L, D_MODEL, 6 * D_MODEL), 0.5 * D_MODEL ** -0.5),
        'b_ada': nrm((L, 6 * D_MODEL), 0.01),
        'g_norm1': 1.0 + nrm((L, D_MODEL), 0.01),
        'g_norm2': 1.0 + nrm((L, D_MODEL), 0.01),
        'w_in': nrm((L, D_MODEL, IN_COLS), D_MODEL ** -0.5),
        'g_mla_q': 1.0 + nrm((L, MLA_Q_RANK), 0.01),
        'w_mla_uq': nrm((L, MLA_Q_RANK, MLA_HEADS * (MLA_NOPE + MLA_ROPE)), MLA_Q_RANK ** -0.5),
        'g_mla_kv': 1.0 + nrm((L, MLA_KV_RANK), 0.01),
        'w_mla_ukv': nrm((L, MLA_KV_RANK, MLA_HEADS * (MLA_NOPE + MLA_V)), MLA_KV_RANK ** -0.5),
        'w_ml_conv': nrm((L, 3, ML_WIDTH), 0.5),
        'b_ml_conv': nrm((L, ML_WIDTH), 0.01),
        'w_ml_q': nrm((L, ML_HEADS, ML_DH, ML_DH), ML_DH ** -0.5),
        'w_ml_k': nrm((L, ML_HEADS, ML_DH, ML_DH), ML_DH ** -0.5),
        'b_ml_gates': gate_base + nrm((L, 4 * ML_HEADS), 0.1),
        'g_ml_out': 1.0 + nrm((L, ML_WIDTH), 0.01),
        'g_gqa_q': 1.0 + nrm((L, GQA_DH), 0.01),
        'g_gqa_k': 1.0 + nrm((L, GQA_DH), 0.01),
        'w_out': nrm((L, MIX_WIDTH, D_MODEL), MIX_WIDTH ** -0.5),
        'w_ff_up': nrm((L, D_MODEL, 2 * D_FF), D_MODEL ** -0.5),
        'w_ff_conv': nrm((L, 3, D_FF), 0.5),
        'b_ff_conv': nrm((L, D_FF), 0.01),
        'w_ff_down': nrm((L, D_FF, D_MODEL), D_FF ** -0.5),
        'g_final': 1.0 + nrm((D_MODEL,), 0.01),
    }


def reference(x_prompt, x_sample, cache_mla_ckv, cache_mla_krope, cache_gqa_k, cache_gqa_v,
              state_mlstm_C, state_mlstm_n, state_mlstm_m, c, c_ctx,
              w_ada, b_ada, g_norm1, g_norm2, w_in, g_mla_q, w_mla_uq, g_mla_kv, w_mla_ukv,
              w_ml_conv, b_ml_conv, w_ml_q, w_ml_k, b_ml_gates, g_ml_out, g_gqa_q, g_gqa_k,
              w_out, w_ff_up, w_ff_conv, b_ff_conv, w_ff_down, g_final):
    params = {
        'w_ada': w_ada, 'b_ada': b_ada, 'g_norm1': g_norm1, 'g_norm2': g_norm2, 'w_in': w_in,
        'g_mla_q': g_mla_q, 'w_mla_uq': w_mla_uq, 'g_mla_kv': g_mla_kv, 'w_mla_ukv': w_mla_ukv,
        'w_ml_conv': w_ml_conv, 'b_ml_conv': b_ml_conv, 'w_ml_q': w_ml_q, 'w_ml_k': w_ml_k,
        'b_ml_gates': b_ml_gates, 'g_ml_out': g_ml_out, 'g_gqa_q': g_gqa_q, 'g_gqa_k': g_gqa_k,
        'w_out': w_out, 'w_ff_up': w_ff_up, 'w_ff_conv': w_ff_conv, 'b_ff_conv': b_ff_conv,
        'w_ff_down': w_ff_down,
    }
    rows, cols = grid_positions(x_sample.shape[1])
    xp = x_prompt
    xs = x_sample
    new_state = [[] for _ in range(7)]
    for l in range(DEPTH):
        lp = {name: arr[l] for name, arr in params.items()}
        sh1, sc1, gt1, sh2, sc2, gt2 = modulation(c_ctx[None, :], lp)
        out, ctx_t = mixer_context(adaln_in(xp, lp['g_norm1'], sh1, sc1), lp)
        xp = xp + gt1 * out
        xp = xp + gt2 * conv_ffn(adaln_in(xp, lp['g_norm2'], sh2, sc2), lp)
        for acc, tns in zip(new_state, ctx_t):
            acc.append(tns)
        cache_l = (cache_mla_ckv[:, l], cache_mla_krope[:, l], cache_gqa_k[:, l], cache_gqa_v[:, l],
                   state_mlstm_C[:, l], state_mlstm_n[:, l], state_mlstm_m[:, l])
        sh1, sc1, gt1, sh2, sc2, gt2 = modulation(c, lp)
        xs = xs + gt1 * mixer_latent(adaln_in(xs, lp['g_norm1'], sh1, sc1), lp, cache_l, rows, cols)
        xs = xs + gt2 * conv_ffn(adaln_in(xs, lp['g_norm2'], sh2, sc2), lp)
    y_prompt = rmsnorm(xp, g_final)
    y_sample = rmsnorm(xs, g_final)
    new_mla_ckv = jnp.stack(new_state[0], axis=1)
    new_mla_krope = jnp.stack(new_state[1], axis=1)
    new_gqa_k = jnp.stack(new_state[2], axis=1)
    new_gqa_v = jnp.stack(new_state[3], axis=1)
    new_mlstm_C = jnp.stack(new_state[4], axis=1)
    new_mlstm_n = jnp.stack(new_state[5], axis=1)
    new_mlstm_m = jnp.stack(new_state[6], axis=1)
    return (y_prompt, y_sample, new_mla_ckv, new_mla_krope, new_gqa_k, new_gqa_v, new_mlstm_C, new_mlstm_n, new_mlstm_m)
```

```python
import math
from contextlib import ExitStack
import numpy as np
import concourse.bass as bass
import concourse.mybir as mybir
from concourse.bass_utils import run_bass_kernel_spmd

F32 = mybir.dt.float32
BF16 = mybir.dt.bfloat16
AF = mybir.ActivationFunctionType
ALU = mybir.AluOpType
AX = mybir.AxisListType

ENGS = ("pe", "act", "dve", "pool", "sp")
NDMA = 24
SEM_ROT = 30000
EPS = 1e-6
DEPTH = 2
D = 1024
DFF = 2816
NX = 3456
DEBUG_ARENA = False


class Dep:
    __slots__ = ("w", "r")

    def __init__(self):
        self.w = None
        self.r = {}


class Buf:
    __slots__ = ("ap", "dep")

    def __init__(self, ap, dep=None):
        self.ap = ap
        self.dep = dep if dep is not None else Dep()

    def __getitem__(self, k):
        return Buf(self.ap[k], self.dep)

    def v(self, ap):
        return Buf(ap, self.dep)


def _ap(x):
    return x.ap if isinstance(x, Buf) else x


class Prog:
    def __init__(self, nc):
        self.nc = nc
        self.ops = {e: [] for e in ENGS}
        self.cnt = {e: 0 for e in ENGS}
        self.epoch = {e: 0 for e in ENGS}
        self.known = {e: {} for e in ENGS}
        self.dma_val = [0] * NDMA
        self.dma_next = 0
        self.dma_next_sw = 0
        self.n_ins = 0

    def _need(self, eng, deps):
        kn = self.known[eng]
        for sk, v in deps.items():
            if kn.get(sk, 0) >= v:
                continue
            if sk[0] == eng and eng == "pe":
                continue
            kn[sk] = v
            self.ops[eng].append(("wait", sk, v))

    @staticmethod
    def _add(deps, tok):
        if tok is None:
            return
        sk, v = tok
        if deps.get(sk, 0) < v:
            deps[sk] = v

    def _collect(self, reads, writes):
        deps = {}
        for b in reads:
            self._add(deps, b.dep.w)
        for b in writes:
            self._add(deps, b.dep.w)
            for sk, v in b.dep.r.items():
                self._add(deps, (sk, v))
        return deps

    def _record(self, tok, reads, writes):
        sk, v = tok
        for b in reads:
            if b.dep.r.get(sk, 0) < v:
                b.dep.r[sk] = v
        for b in writes:
            b.dep.w = tok
            b.dep.r = {}

    def op(self, eng, fn, reads=(), writes=()):
        reads = [b for b in reads if isinstance(b, Buf)]
        writes = [b for b in writes if isinstance(b, Buf)]
        deps = self._collect(reads, writes)
        self._need(eng, deps)
        if self.cnt[eng] >= SEM_ROT:
            self.epoch[eng] += 1
            self.cnt[eng] = 0
        sk = (eng, self.epoch[eng])
        self.cnt[eng] += 1
        v = self.cnt[eng]
        self.ops[eng].append(("ins", fn, sk, 1))
        if eng == "pe":
            self.known[eng][sk] = v
        self._record((sk, v), reads, writes)
        self.n_ins += 1

    def dma(self, q, out, in_, reads=(), writes=()):
        reads = [b for b in reads if isinstance(b, Buf)]
        writes = [b for b in writes if isinstance(b, Buf)]
        deps = self._collect(reads, writes)
        half = NDMA // 2
        if q == "pool":
            j = half + self.dma_next_sw
            self.dma_next_sw = (self.dma_next_sw + 1) % half
        else:
            j = self.dma_next
            self.dma_next = (j + 1) % half
        sk = ("dma", j)
        if self.dma_val[j] > 0:
            self._add(deps, (sk, self.dma_val[j]))
        self._need(q, deps)
        self.dma_val[j] += 16
        v = self.dma_val[j]
        o, i = _ap(out), _ap(in_)
        def _issue(e, o=o, i=i):
            try:
                return e.dma_start(out=o, in_=i)
            except ValueError:
                return e.dma_start(out=o, in_=i, allow_slow_non_contiguous=True)
        self.ops[q].append(("ins", _issue, sk, 16))
        self._record((sk, v), reads, writes)
        self.n_ins += 1

    def barrier(self):
        deps = {}
        for e in ENGS:
            for ep in range(self.epoch[e] + 1):
                val = self.cnt[e] if ep == self.epoch[e] else SEM_ROT
                if val > 0:
                    deps[(e, ep)] = val
        for j in range(NDMA):
            if self.dma_val[j] > 0:
                deps[("dma", j)] = self.dma_val[j]
        for e in ENGS:
            kn = self.known[e]
            for sk, v in deps.items():
                if sk[0] == e and e == "pe":
                    continue
                if kn.get(sk, 0) >= v:
                    continue
                kn[sk] = v
                self.ops[e].append(("wait", sk, v))

    def emit(self):
        nc = self.nc
        semkeys = set()
        for e in ENGS:
            for it in self.ops[e]:
                semkeys.add(it[1] if it[0] == "wait" else it[2])
        semkeys = sorted(semkeys, key=str)
        with ExitStack() as st:
            sems = {}
            for sk in semkeys:
                sems[sk] = st.enter_context(nc.semaphore(f"s_{sk[0]}_{sk[1]}"))
            block = st.enter_context(nc.Block())
            ops = self.ops
            needed = {}
            for e in ENGS:
                for it in ops[e]:
                    if it[0] == "wait" and it[1][0] != "dma":
                        needed.setdefault(it[1], set()).add(it[2])
            remap = {}
            for e in ENGS:
                idx = {}
                cnt = {}
                for it in ops[e]:
                    if it[0] == "ins" and it[2][0] != "dma":
                        sk = it[2]
                        idx[sk] = idx.get(sk, 0) + 1
                        if idx[sk] in needed.get(sk, ()):
                            cnt[sk] = cnt.get(sk, 0) + 1
                            remap[(sk, idx[sk])] = cnt[sk]
            self.n_inc = len(remap)

            def replay(engine, lst):
                idx = {}
                for it in lst:
                    if it[0] == "wait":
                        if it[1][0] == "dma":
                            engine.wait_ge(sems[it[1]], it[2])
                        else:
                            engine.wait_ge(sems[it[1]], remap[(it[1], it[2])])
                    else:
                        sk = it[2]
                        if sk[0] == "dma":
                            it[1](engine).then_inc(sems[sk], it[3])
                        else:
                            idx[sk] = idx.get(sk, 0) + 1
                            ins = it[1](engine)
                            if (sk, idx[sk]) in remap:
                                ins.then_inc(sems[sk], 1)

            @block.tensor
            def _(e):
                replay(e, ops["pe"])

            @block.scalar
            def _(e):
                replay(e, ops["act"])

            @block.vector
            def _(e):
                replay(e, ops["dve"])

            @block.gpsimd
            def _(e):
                replay(e, ops["pool"])

            @block.sync
            def _(e):
                replay(e, ops["sp"])

    def mm(self, out, lhsT, rhs, start=True, stop=True):
        self.op("pe", lambda e: e.matmul(out.ap, lhsT.ap, rhs.ap, start=start, stop=stop),
                reads=[lhsT, rhs], writes=[out])

    def tr(self, out, in_, ident):
        self.op("pe", lambda e: e.transpose(out.ap, in_.ap, ident.ap), reads=[in_, ident], writes=[out])

    def act(self, out, in_, func, bias=0.0, scale=1.0, accum=None):
        kw = {}
        if accum is not None:
            kw["accum_out"] = accum.ap
        b, s = _ap(bias), _ap(scale)
        self.op("act", lambda e: e.activation(out.ap, in_.ap, func, bias=b, scale=s, **kw),
                reads=[in_, bias, scale], writes=[out] + ([accum] if accum is not None else []))

    def tt(self, eng, out, a, b, op):
        self.op(eng, lambda e: e.tensor_tensor(out.ap, a.ap, b.ap, op), reads=[a, b], writes=[out])

    def ts(self, eng, out, a, s1, op0, s2=None, op1=None):
        x1, x2 = _ap(s1), _ap(s2)
        if op1 is None:
            self.op(eng, lambda e: e.tensor_scalar(out.ap, a.ap, x1, None, op0), reads=[a, s1], writes=[out])
        else:
            self.op(eng, lambda e: e.tensor_scalar(out.ap, a.ap, x1, x2, op0, op1), reads=[a, s1, s2], writes=[out])

    def stt(self, eng, out, in0, scalar, in1, op0, op1):
        sc = _ap(scalar)
        self.op(eng, lambda e: e.scalar_tensor_tensor(out.ap, in0.ap, sc, in1.ap, op0, op1),
                reads=[in0, scalar, in1], writes=[out])

    def copy(self, eng, out, in_):
        if eng == "act":
            self.op("act", lambda e: e.activation(out.ap, in_.ap, AF.Copy), reads=[in_], writes=[out])
        else:
            self.op(eng, lambda e: e.tensor_copy(out.ap, in_.ap), reads=[in_], writes=[out])

    def memset(self, eng, out, val):
        self.op(eng, lambda e: e.memset(out.ap, val), writes=[out])

    def recip(self, out, in_):
        self.op("dve", lambda e: e.reciprocal(out.ap, in_.ap), reads=[in_], writes=[out])

    def scan(self, out, d0, d1, init, op0, op1):
        iv = _ap(init)
        self.op("dve", lambda e: e.tensor_tensor_scan(out.ap, d0.ap, d1.ap, iv, op0, op1),
                reads=[d0, d1, init], writes=[out])


def _rope_tables():
    T = 2048
    t = np.arange(T)
    rows = (t // 64).astype(np.float32)
    cols = (t % 64).astype(np.float32)

    def tab(d):
        h = d // 2
        q = h // 2
        inv = (10000.0 ** (-np.arange(0, h, 2, dtype=np.float32) / h)).astype(np.float32)
        C = np.zeros((d, T), np.float32)
        S = np.zeros((d, T), np.float32)
        for g, pos in enumerate((rows, cols)):
            ang = pos[None, :] * inv[:, None]
            c, s = np.cos(ang).astype(np.float32), np.sin(ang).astype(np.float32)
            C[g * h:g * h + q] = c
            C[g * h + q:g * h + h] = c
            S[g * h:g * h + q] = -s
            S[g * h + q:g * h + h] = s
        return C, S

    Cm, Sm = tab(32)
    Cg, Sg = tab(64)
    rm = np.zeros((128, 2, T), np.float32)
    rm[64:96, 0] = Cm
    rm[64:96, 1] = Sm
    rg = np.zeros((128, 2, T), np.float32)
    rg[0:64, 0] = Cg
    rg[64:128, 0] = Cg
    rg[0:64, 1] = Sg
    rg[64:128, 1] = Sg
    return rm, rg


def _swap_idx(d):
    h = d // 2
    q = h // 2
    idx = np.arange(d)
    out = idx.copy()
    for g in range(2):
        out[g * h:g * h + q] = idx[g * h + q:g * h + h]
        out[g * h + q:g * h + h] = idx[g * h:g * h + q]
    return out


def _layout_weights(w_in, w_mla_uq, w_mla_ukv, g_gqa_q, g_gqa_k, b_ml_gates):
    L = w_in.shape[0]
    o_cq, o_ckv, o_kr, o_u, o_v, o_o, o_g, o_qg, o_kg, o_vg = 0, 256, 384, 416, 928, 1440, 1952, 1968, 2224, 2352
    sw32 = _swap_idx(32)
    sw64 = _swap_idx(64)
    wx = np.zeros((L, D, NX), np.float32)
    wx[:, :, 0:256] = w_in[:, :, o_cq:o_cq + 256]
    wx[:, :, 256:384] = w_in[:, :, o_ckv:o_ckv + 128]
    wx[:, :, 384 + 64:480] = w_in[:, :, o_kr:o_kr + 32]
    wx[:, :, 480 + 64:576] = w_in[:, :, o_kr + sw32]
    qcols = lambda h: np.arange(o_qg + h * 64, o_qg + (h + 1) * 64)
    qA = np.concatenate([qcols(0), qcols(2)])
    qB = np.concatenate([qcols(1), qcols(3)])
    qAs = np.concatenate([qcols(0)[sw64], qcols(2)[sw64]])
    qBs = np.concatenate([qcols(1)[sw64], qcols(3)[sw64]])
    wx[:, :, 576:704] = w_in[:, :, qA]
    wx[:, :, 704:832] = w_in[:, :, qB]
    wx[:, :, 832:960] = w_in[:, :, qAs]
    wx[:, :, 960:1088] = w_in[:, :, qBs]
    kc = np.arange(o_kg, o_kg + 128)
    kcs = np.concatenate([kc[0:64][sw64], kc[64:128][sw64]])
    wx[:, :, 1088:1216] = w_in[:, :, kc]
    wx[:, :, 1216:1344] = w_in[:, :, kcs]
    wx[:, :, 1344:1472] = w_in[:, :, o_vg:o_vg + 128]
    for h in range(4):
        b = 1472 + h * 384
        wx[:, :, b:b + 128] = w_in[:, :, o_u + h * 128:o_u + (h + 1) * 128]
        wx[:, :, b + 128:b + 256] = w_in[:, :, o_v + h * 128:o_v + (h + 1) * 128]
        wx[:, :, b + 256:b + 384] = w_in[:, :, o_o + h * 128:o_o + (h + 1) * 128]
    wx[:, :, 3008:3024] = w_in[:, :, o_g:o_g + 16]
    wx[:, :, 3040:3168] = w_in[:, :, o_ckv:o_ckv + 128]
    wx[:, :, 3168:3200] = w_in[:, :, o_kr:o_kr + 32]
    wx[:, :, 3200:3328] = w_in[:, :, o_kg:o_kg + 128]
    wx[:, :, 3328:3456] = w_in[:, :, o_vg:o_vg + 128]
    uqx = np.zeros((L, 256, 4, 192), np.float32)
    for h in range(4):
        uqx[:, :, h, 0:96] = w_mla_uq[:, :, h * 96:(h + 1) * 96]
        uqx[:, :, h, 96 + 64:192] = w_mla_uq[:, :, h * 96 + 64 + sw32]
    uqx = uqx.reshape(L, 256, 768)
    ukvx = np.zeros((L, 128, 512), np.float32)
    for h in range(4):
        ukvx[:, :, h * 64:(h + 1) * 64] = w_mla_ukv[:, :, h * 128:h * 128 + 64]
        ukvx[:, :, 256 + h * 64:256 + (h + 1) * 64] = w_mla_ukv[:, :, h * 128 + 64:(h + 1) * 128]
    gq = np.stack([np.concatenate([g_gqa_q, g_gqa_q], 1), np.concatenate([g_gqa_q[:, sw64], g_gqa_q[:, sw64]], 1),
                   np.concatenate([g_gqa_k, g_gqa_k], 1), np.concatenate([g_gqa_k[:, sw64], g_gqa_k[:, sw64]], 1)], 2)
    bg = np.ascontiguousarray(b_ml_gates.reshape(L, 4, 4).transpose(0, 2, 1))
    return wx, uqx, ukvx, np.ascontiguousarray(gq.astype(np.float32)), bg.astype(np.float32)


def _consts():
    c = {}
    c["k_ident"] = np.eye(128, dtype=np.float32)
    sel = np.zeros((4, 4, 128), np.float32)
    for h in range(4):
        sel[h, h, :] = 1.0
    c["k_sel"] = sel
    c["k_nsel"] = -sel
    c["k_noh"] = (-np.eye(4)).astype(np.float32)
    c["k_oh"] = np.eye(4, dtype=np.float32)
    s = np.arange(128)[:, None]
    t = np.arange(128)[None, :]
    mf = np.where(s <= t, 0.0, 1e4).astype(np.float32)
    mb = np.where(s >= t, 0.0, 1e4).astype(np.float32)
    c["k_mask"] = np.stack([mf, mb], 1)
    rm, rg = _rope_tables()
    c["k_ropem"] = rm
    c["k_ropeg"] = rg
    return c


def build_program(do_p=True, do_s=True, debug=None):
    nc = bass.Bass("TRN2", target_bir_lowering=False)
    P = Prog(nc)

    def din(name, shape):
        return nc.dram_tensor(name, list(shape), F32, kind="ExternalInput").ap()

    def dout(name, shape):
        return nc.dram_tensor(name, list(shape), F32, kind="ExternalOutput").ap()

    I = {}
    I["xp"] = din("xp", (1024, D))
    I["xs"] = din("xs", (2048, D))
    I["c_ckv"] = din("c_ckv", (DEPTH, 256, 128))
    I["c_kr"] = din("c_kr", (DEPTH, 256, 32))
    I["c_gk"] = din("c_gk", (DEPTH, 256, 128))
    I["c_gv"] = din("c_gv", (DEPTH, 256, 128))
    I["C0"] = din("C0", (DEPTH, 2, 4, 128, 128))
    I["n0"] = din("n0", (DEPTH, 2, 4, 128))
    I["m0"] = din("m0", (DEPTH, 2, 4))
    I["cvec"] = din("cvec", (2, D))
    I["w_ada"] = din("w_ada", (DEPTH, D, 6 * D))
    I["b_ada"] = din("b_ada", (DEPTH, 6 * D))
    I["g_norm1"] = din("g_norm1", (DEPTH, D))
    I["g_norm2"] = din("g_norm2", (DEPTH, D))
    I["wx"] = din("wx", (DEPTH, D, NX))
    I["g_mla_q"] = din("g_mla_q", (DEPTH, 256))
    I["uqx"] = din("uqx", (DEPTH, 256, 768))
    I["g_mla_kv"] = din("g_mla_kv", (DEPTH, 128))
    I["ukvx"] = din("ukvx", (DEPTH, 128, 512))
    I["w_ml_conv"] = din("w_ml_conv", (DEPTH, 3, 512))
    I["b_ml_conv"] = din("b_ml_conv", (DEPTH, 512))
    I["w_ml_q"] = din("w_ml_q", (DEPTH, 4, 128, 128))
    I["w_ml_k"] = din("w_ml_k", (DEPTH, 4, 128, 128))
    I["bg"] = din("bg", (DEPTH, 4, 4))
    I["g_ml_out"] = din("g_ml_out", (DEPTH, 512))
    I["gq"] = din("gq", (DEPTH, 128, 4))
    I["g_gqa_k"] = din("g_gqa_k", (DEPTH, 64))
    I["w_out"] = din("w_out", (DEPTH, D, D))
    I["w_ff_up"] = din("w_ff_up", (DEPTH, D, 2 * DFF))
    I["w_ff_conv"] = din("w_ff_conv", (DEPTH, 3, DFF))
    I["b_ff_conv"] = din("b_ff_conv", (DEPTH, DFF))
    I["w_ff_down"] = din("w_ff_down", (DEPTH, DFF, D))
    I["g_final"] = din("g_final", (D,))
    I["k_ident"] = din("k_ident", (128, 128))
    I["k_sel"] = din("k_sel", (4, 4, 128))
    I["k_noh"] = din("k_noh", (4, 4))
    I["k_nsel"] = din("k_nsel", (4, 4, 128))
    I["k_oh"] = din("k_oh", (4, 4))
    I["k_mask"] = din("k_mask", (128, 2, 128))
    I["k_ropem"] = din("k_ropem", (128, 2, 2048))
    I["k_ropeg"] = din("k_ropeg", (128, 2, 2048))
    O = {}
    O["yp"] = dout("yp", (1024, D))
    O["ys"] = dout("ys", (2048, D))
    O["n_ckv"] = dout("n_ckv", (4, DEPTH, 256, 128))
    O["n_kr"] = dout("n_kr", (4, DEPTH, 256, 32))
    O["n_k"] = dout("n_k", (4, DEPTH, 256, 128))
    O["n_v"] = dout("n_v", (4, DEPTH, 256, 128))
    O["n_C"] = dout("n_C", (4, DEPTH, 2, 4, 128, 128))
    O["n_n"] = dout("n_n", (4, DEPTH, 2, 4, 128))
    O["n_m"] = dout("n_m", (4, DEPTH, 2, 4))
    if debug:
        for nm, shp in debug.items():
            O[nm] = dout(nm, shp)

    st = ExitStack()
    with st:
        def sbt(name, shape, dt):
            return Buf(st.enter_context(nc.sbuf_tensor(name, list(shape), dt)).ap())

        xT = sbt("xT", (128, 8, 2048), F32)
        yT = sbt("yT", (128, 8, 2048), BF16)
        ident = sbt("ident", (128, 128), F32)
        identb = sbt("identb", (128, 128), BF16)
        onesb = sbt("onesb", (128, 128), BF16)
        bd64 = sbt("bd64", (128, 128), BF16)
        sel = sbt("sel", (4, 2, 128), F32)
        nsel = sbt("nsel", (4, 1, 128), F32)
        oh = sbt("oh", (4, 4), F32)
        modT = sbt("modT", (128, DEPTH, 6, 8, 2), F32)
        gn = sbt("gn", (128, DEPTH, 2, 8), F32)
        gfin = sbt("gfin", (128, 8), F32)
        AB = sbt("AB", (128, 4, 8), F32)
        m0t = sbt("m0t", (4, DEPTH * 2), F32)
        AW = 27500
        arena = st.enter_context(nc.sbuf_tensor("arena", [128, AW], F32)).ap()
        gsc = Buf(nc.dram_tensor("gsc", [2, 4, 4, 2048], F32, kind="Internal").ap())
        psb = [Buf(st.enter_context(nc.psum_tensor(f"ps{i}", [128, 512], F32)).ap()) for i in range(8)]
        apos = [0]
        peak = [0]
        pctr = [0]

        def areset():
            P.barrier()
            peak[0] = max(peak[0], apos[0])
            if DEBUG_ARENA:
                print('arena used', apos[0])
            apos[0] = 0

        def alloc(shape, dt):
            n = int(np.prod(shape[1:]))
            words = n if dt == F32 else (n + 1) // 2
            words = (words + 3) // 4 * 4
            off = apos[0]
            apos[0] += words
            assert apos[0] <= AW, f"arena overflow {apos[0]} > {AW}"
            ap = arena[:, off:off + words]
            if dt == BF16:
                ap = ap.bitcast(BF16)
            ap = ap[:, 0:n]
            if len(shape) == 3:
                ap = ap.rearrange("p (a b) -> p a b", a=shape[1])
            elif len(shape) == 4:
                ap = ap.rearrange("p (a b c) -> p a b c", a=shape[1], b=shape[2])
            if shape[0] < 128:
                ap = ap[0:shape[0]]
            return Buf(ap)

        wscr = {}

        def wload_multi(items, is_s):
            if not (do_p and do_s):
                for key, tile, src in items:
                    P.dma("pool", tile, src, writes=[tile])
                return
            for key, tile, src in items:
                if key not in wscr:
                    nm = "ws_" + "_".join(str(k_) for k_ in key)
                    wscr[key] = Buf(nc.dram_tensor(nm, list(tile.ap.shape), BF16, kind="Internal").ap())
            if not is_s:
                for key, tile, src in items:
                    P.dma("pool", tile, src, writes=[tile])
                for key, tile, src in items:
                    P.dma("sp", wscr[key], tile, reads=[tile], writes=[wscr[key]])
            else:
                for key, tile, src in items:
                    P.dma("sp", tile, wscr[key], reads=[wscr[key]], writes=[tile])

        def wload(key, tile, src, is_s):
            wload_multi([(key, tile, src)], is_s)

        def psum():
            b = psb[pctr[0] % 8]
            pctr[0] += 1
            return b

        def pbf(b):
            return b.v(b.ap.bitcast(BF16))

        P.dma("sp", ident, I["k_ident"], writes=[ident])
        P.dma("pool", identb, I["k_ident"], writes=[identb])
        P.dma("sp", sel, I["k_sel"][:, 0:2, :], writes=[sel])
        P.dma("sp", nsel, I["k_nsel"][:, 3:4, :], writes=[nsel])
        P.dma("sp", oh, I["k_oh"], writes=[oh])
        P.memset("pool", onesb, 1.0)
        P.memset("pool", bd64, 0.0)
        P.memset("pool", bd64[0:64, 0:64], 1.0)
        P.memset("pool", bd64[64:128, 64:128], 1.0)
        for l in range(DEPTH):
            P.dma("sp", gn[:, l, 0, :], I["g_norm1"][l].rearrange("(k p) -> p k", p=128), writes=[gn])
            P.dma("sp", gn[:, l, 1, :], I["g_norm2"][l].rearrange("(k p) -> p k", p=128), writes=[gn])
        P.dma("sp", gfin, I["g_final"].rearrange("(k p) -> p k", p=128), writes=[gfin])
        P.dma("sp", m0t, I["m0"].rearrange("l d h -> h (l d)"), writes=[m0t])

        mod_done = [False]

        def do_modulation():
            mod_done[0] = True
            cT = alloc((128, 8, 2), F32)
            scT = alloc((128, 8, 2), BF16)
            bT = alloc((128, DEPTH, 48), F32)
            for w in range(2):
                P.dma("sp", cT[:, :, w], I["cvec"][w].rearrange("(k p) -> p k", p=128), writes=[cT])
            for l in range(DEPTH):
                P.dma("sp", bT[:, l, :], I["b_ada"][l].rearrange("(j p) -> p j", p=128), writes=[bT])
            P.act(scT, cT, AF.Silu)
            wq = [alloc((128, 8, 1536), BF16) for _ in range(2)]
            qi_ = 0
            for l in range(DEPTH):
                pm = psum()
                pmv = pm.v(pm.ap[:, 0:96].rearrange("p (j w) -> p j w", w=2))
                for qd in range(4):
                    wb = wq[qi_ % 2]
                    qi_ += 1
                    P.dma("pool", wb, I["w_ada"][l][:, qd * 1536:(qd + 1) * 1536].rearrange("(k p) c -> p k c", p=128), writes=[wb])
                    for jc in range(12):
                        j = qd * 12 + jc
                        for kc in range(8):
                            P.mm(pmv[:, j, :], wb[:, kc, jc * 128:(jc + 1) * 128], scT[:, kc, :], start=(kc == 0), stop=(kc == 7))
                for w in range(2):
                    P.tt("dve", modT.v(modT.ap[:, l, :, :, w].rearrange("p i k -> p (i k)")), pmv[:, :, w], bT[:, l, :], ALU.add)


        def run_group(gname, which, xin, yout, nseq, L, rope, ctx):
            T = nseq * L
            NB = T // 512
            NCH = T // 128
            nch_seq = L // 128
            Lk = L + (256 if ctx else 0)
            nkt = Lk // 128
            qblk = 256
            nqt = qblk // 128

            def X(kc, c0, c1):
                return xT[:, kc, c0:c1]

            def Y(kc, c0, c1):
                return yT[:, kc, c0:c1]

            areset()
            stg = [alloc((128, D), F32) for _ in range(2)]
            for tt_ in range(NCH):
                sg = stg[tt_ % 2]
                P.dma("sp", sg, xin[tt_ * 128:(tt_ + 1) * 128, :], writes=[sg])
                for half in range(2):
                    pp = psum()
                    for k4 in range(4):
                        kc = half * 4 + k4
                        P.tr(pp[:, k4 * 128:(k4 + 1) * 128], sg[:, kc * 128:(kc + 1) * 128], ident)
                    P.copy("act" if half == 0 else "dve",
                           xT.v(xT.ap[:, half * 4:half * 4 + 4, tt_ * 128:(tt_ + 1) * 128]),
                           pp.v(pp.ap.rearrange("p (a b) -> p a b", a=4)))
            if not mod_done[0]:
                do_modulation()

            def norm_mod(Acol, Bcol):
                sq = [alloc((128, 8, 512), BF16) for _ in range(2)]
                rs = [alloc((128, 512), F32) for _ in range(2)]
                tmp = [alloc((128, 512), F32) for _ in range(3)]
                ti = 0
                for tb in range(NB):
                    c0, c1 = tb * 512, (tb + 1) * 512
                    s_ = sq[tb % 2]
                    r_ = rs[tb % 2]
                    P.tt("pool", s_, xT[:, :, c0:c1], xT[:, :, c0:c1], ALU.mult)
                    pss = psum()
                    for kc in range(8):
                        P.mm(pss, onesb, s_[:, kc, :], start=(kc == 0), stop=(kc == 7))
                    P.act(r_, pss, AF.Ln, bias=float(D * EPS))
                    P.act(r_, r_, AF.Exp, scale=-0.5)
                    for kc in range(8):
                        t_ = tmp[ti % 3]
                        ti += 1
                        P.stt("dve", t_, X(kc, c0, c1), Acol(kc), r_, ALU.mult, ALU.mult)
                        if Bcol is None:
                            P.copy("act", Y(kc, c0, c1), t_)
                        else:
                            P.act(Y(kc, c0, c1), t_, AF.Identity, bias=Bcol(kc))

            def mcol(l, i, kc):
                return modT[:, l, i, kc, which:which + 1]

            def outproj(l, oT, row0, nkc):
                wo = alloc((128, nkc, D), BF16)
                wload(("wo", l, row0), wo, I["w_out"][l][row0:row0 + nkc * 128, :].rearrange("(k p) c -> p k c", p=128), ctx)
                for tb in range(NB):
                    c0, c1 = tb * 512, (tb + 1) * 512
                    for m in range(8):
                        pp = psum()
                        for kc in range(nkc):
                            P.mm(pp, wo[:, kc, m * 128:(m + 1) * 128], oT[:, kc, c0:c1], start=(kc == 0), stop=(kc == nkc - 1))
                        P.stt("dve", X(m, c0, c1), pp, mcol(l, 2, m), X(m, c0, c1), ALU.mult, ALU.add)

            for l in range(DEPTH):
                areset()
                for i_, (gi, si) in enumerate(((0, 1), (1, 4))):
                    P.ts("dve", AB[:, i_, :], modT.v(modT.ap[:, l, si, :, which]), 1.0, ALU.add, 32.0, ALU.mult)
                    P.tt("dve", AB[:, i_, :], AB[:, i_, :], gn[:, l, gi, :], ALU.mult)
                norm_mod(lambda kc: AB[:, 0, kc:kc + 1], lambda kc: mcol(l, 0, kc))

                if not ctx:
                    areset()
                    wtm = alloc((128, 8, 416), BF16)
                    P.dma("pool", wtm, I["wx"][l][:, 3040:3456].rearrange("(k p) c -> p k c", p=128), writes=[wtm])
                    gkv_bc = alloc((128, 128), F32)
                    gk_bc = alloc((128, 64), F32)
                    P.dma("sp", gkv_bc, I["g_mla_kv"][l].partition_broadcast(128), writes=[gkv_bc])
                    P.dma("sp", gk_bc, I["g_gqa_k"][l].partition_broadcast(128), writes=[gk_bc])
                    stg2 = [alloc((128, 416), F32) for _ in range(2)]
                    junk = alloc((128, 128), F32)
                    st3 = [alloc((128, 4), F32) for _ in range(2)]
                    for tt_ in range(NCH):
                        sq_, so = stg2[tt_ % 2], st3[tt_ % 2]
                        s_i, tk = divmod(tt_, nch_seq)
                        pp = psum()
                        for kc in range(8):
                            P.mm(pp[:, 0:416], Y(kc, tt_ * 128, (tt_ + 1) * 128), wtm[:, kc, :], start=(kc == 0), stop=(kc == 7))
                        for gi_, (a, b) in enumerate(((0, 128), (160, 224), (224, 288))):
                            P.act(junk[:, 0:b - a], pp[:, a:b], AF.Square, accum=so[:, gi_:gi_ + 1])
                        P.act(so[:, 0:1], so[:, 0:1], AF.Ln, bias=EPS, scale=1.0 / 128)
                        P.act(so[:, 1:3], so[:, 1:3], AF.Ln, bias=EPS, scale=1.0 / 64)
                        P.act(so[:, 0:3], so[:, 0:3], AF.Exp, scale=-0.5)
                        P.stt("dve", sq_[:, 0:128], pp[:, 0:128], so[:, 0:1], gkv_bc, ALU.mult, ALU.mult)
                        P.copy("act", sq_[:, 128:160], pp[:, 128:160])
                        P.stt("dve", sq_[:, 160:224], pp[:, 160:224], so[:, 1:2], gk_bc, ALU.mult, ALU.mult)
                        P.stt("dve", sq_[:, 224:288], pp[:, 224:288], so[:, 2:3], gk_bc, ALU.mult, ALU.mult)
                        P.copy("act", sq_[:, 288:416], pp[:, 288:416])
                        r0, r1 = tk * 128, (tk + 1) * 128
                        P.dma("sp", O["n_ckv"][s_i, l, r0:r1, :], sq_[:, 0:128], reads=[sq_])
                        P.dma("sp", O["n_kr"][s_i, l, r0:r1, :], sq_[:, 128:160], reads=[sq_])
                        P.dma("sp", O["n_k"][s_i, l, r0:r1, :], sq_[:, 160:288], reads=[sq_])
                        P.dma("sp", O["n_v"][s_i, l, r0:r1, :], sq_[:, 288:416], reads=[sq_])

                areset()
                wA = alloc((128, 8, 576), BF16)
                wload(("wA", l), wA, I["wx"][l][:, 0:576].rearrange("(k p) c -> p k c", p=128), ctx)
                wuq = alloc((128, 2, 768), BF16)
                wload(("wuq", l), wuq, I["uqx"][l].rearrange("(k p) c -> p k c", p=128), ctx)
                wukv = alloc((128, 512), BF16)
                wload(("wukv", l), wukv, I["ukvx"][l], ctx)
                gqc = alloc((128, 2), F32)
                P.dma("sp", gqc, I["g_mla_q"][l].rearrange("(k p) -> p k", p=128), writes=[gqc])
                gkvc = alloc((128, 1), F32)
                P.dma("sp", gkvc, I["g_mla_kv"][l].rearrange("(p o) -> p o", o=1), writes=[gkvc])
                rtabs = [alloc((128, 2, 512), F32) for _ in range(2)]
                rti = [0]

                def load_rt(src, t0, n):
                    r_ = rtabs[rti[0] % 2]
                    rti[0] += 1
                    P.dma("sp", r_[:, :, 0:n], I[src][:, :, t0:t0 + n], writes=[r_])
                    return r_
                ckvT = alloc((128, nseq, Lk), BF16)
                khT = [alloc((96, nseq, Lk), BF16) for _ in range(4)]
                krT = khT[0]
                vaug = alloc((128, nseq * nkt, 4, 65), BF16)
                P.memset("pool", vaug[:, :, :, 64:65], 1.0)
                sqb = [alloc((128, 512), BF16) for _ in range(2)]
                rsb = [alloc((128, 512), F32) for _ in range(2)]
                tA = [alloc((128, 512), F32)] * 2
                tB = [alloc((128, 512), F32)] * 2
                for tb in range(NB):
                    c0, c1 = tb * 512, (tb + 1) * 512
                    nsq = 512 // L if L < 512 else 1
                    s0 = c0 // L
                    o0 = c0 % L

                    def dst(tile):
                        if L >= 512:
                            return tile.v(tile.ap[:, s0, o0:o0 + 512])
                        return tile.v(tile.ap[:, s0:s0 + nsq, 0:L])

                    def as3(b_):
                        if L >= 512:
                            return b_
                        return b_.v(b_.ap.rearrange("p (a b) -> p a b", a=nsq))
                    pr = psum()
                    for kc in range(8):
                        P.mm(pr, wA[:, kc, 256:384], Y(kc, c0, c1), start=(kc == 0), stop=(kc == 7))
                    sq_, r_ = sqb[tb % 2], rsb[tb % 2]
                    P.act(sq_, pr, AF.Square)
                    pss = psum()
                    P.mm(pss, onesb, sq_)
                    P.act(r_, pss, AF.Ln, bias=EPS, scale=1.0 / 128)
                    P.act(r_, r_, AF.Exp, scale=-0.5)
                    P.stt("dve", dst(ckvT), as3(pr), gkvc[:, 0:1], as3(r_), ALU.mult, ALU.mult)
                    p1 = psum()
                    for kc in range(8):
                        P.mm(p1[0:96, :], wA[:, kc, 384:480], Y(kc, c0, c1), start=(kc == 0), stop=(kc == 7))
                    if rope:
                        p2 = psum()
                        for kc in range(8):
                            P.mm(p2[0:96, :], wA[:, kc, 480:576], Y(kc, c0, c1), start=(kc == 0), stop=(kc == 7))
                        a_, b_ = tA[tb % 2], tB[tb % 2]
                        rtab = load_rt("k_ropem", c0, 512)
                        P.tt("dve", a_[64:96, :], p1[64:96, :], rtab[64:96, 0, :], ALU.mult)
                        P.tt("dve", b_[64:96, :], p2[64:96, :], rtab[64:96, 1, :], ALU.mult)
                        P.tt("pool", dst(krT)[64:96], a_[64:96, :], b_[64:96, :], ALU.add)
                    else:
                        P.copy("act", dst(krT)[64:96], as3(p1)[64:96])
                if ctx:
                    cst = alloc((128, 2, 128), F32)
                    P.dma("sp", cst, I["c_ckv"][l].rearrange("(t p) c -> p t c", p=128), writes=[cst])
                    cs2 = alloc((128, 2, 96), F32)
                    P.memset("pool", cs2, 0.0)
                    P.dma("sp", cs2[:, :, 64:96], I["c_kr"][l].rearrange("(t p) c -> p t c", p=128), writes=[cs2])
                    for t2 in range(2):
                        pp = psum()
                        P.tr(pp[:, 0:128], cst[:, t2, :], ident)
                        P.copy("act", ckvT[:, 0, L + t2 * 128:L + (t2 + 1) * 128], pp[:, 0:128])
                        pp = psum()
                        P.tr(pp[0:96, 0:128], cs2[:, t2, :], ident)
                        P.copy("act", krT[64:96, 0, L + t2 * 128:L + (t2 + 1) * 128], pp[64:96, 0:128])
                for s in range(nseq):
                    for kt in range(nkt):
                        pp = psum()
                        P.mm(pp[:, 0:256], ckvT[:, s, kt * 128:(kt + 1) * 128], wukv[:, 256:512])
                        P.copy("act" if kt % 2 else "dve", vaug[:, s * nkt + kt, :, 0:64],
                               pp.v(pp.ap[:, 0:256].rearrange("p (h d) -> p h d", h=4)))
                for h in range(4):
                    if h > 0:
                        P.copy("pool", khT[h][64:96], krT[64:96])
                    for s in range(nseq):
                        for k0 in range(0, Lk, 512):
                            k1 = min(Lk, k0 + 512)
                            pp = psum()
                            P.mm(pp[0:64, 0:k1 - k0], wukv[:, h * 64:(h + 1) * 64], ckvT[:, s, k0:k1])
                            P.copy("act" if h % 2 else "dve", khT[h][0:64, s, k0:k1], pp[0:64, 0:k1 - k0])
                oT = alloc((128, 2, T), BF16)
                cqn = [alloc((128, 2, qblk), BF16) for _ in range(2)]
                qh = [alloc((96, qblk), BF16) for _ in range(2)]
                PT = [alloc((128, nkt, qblk), BF16) for _ in range(2)]
                otok = [alloc((128, nqt, 256), BF16) for _ in range(2)]
                rcp = [alloc((128, 4), F32) for _ in range(2)]
                sq2 = [alloc((128, 2, qblk), BF16) for _ in range(2)]
                units = [(s, q0, h) for s in range(nseq) for q0 in range(0, L, qblk) for h in range(4)]
                blk = {}
                uctx = {}
                cnt = {"qb": 0, "h": 0}

                blocks = [(s, q0) for s in range(nseq) for q0 in range(0, L, qblk)]

                def mla_pro(s, q0):
                    g0 = s * L + q0
                    if True:
                        qbi = cnt["qb"]
                        cnt["qb"] += 1
                        cq_, ot_ = cqn[qbi % 2], otok[qbi % 2]
                        s2_, r_ = sq2[qbi % 2], rsb[qbi % 2]
                        rtq = load_rt("k_ropem", q0, qblk) if rope else None
                        pc = [psum(), psum()]
                        for c in range(2):
                            for kc in range(8):
                                P.mm(pc[c][:, 0:qblk], wA[:, kc, c * 128:(c + 1) * 128], Y(kc, g0, g0 + qblk), start=(kc == 0), stop=(kc == 7))
                            P.act(s2_[:, c, :], pc[c][:, 0:qblk], AF.Square)
                        pss = psum()
                        for c in range(2):
                            P.mm(pss[:, 0:qblk], onesb, s2_[:, c, :], start=(c == 0), stop=(c == 1))
                        P.act(r_[:, 0:qblk], pss[:, 0:qblk], AF.Ln, bias=EPS, scale=1.0 / 256)
                        P.act(r_[:, 0:qblk], r_[:, 0:qblk], AF.Exp, scale=-0.5)
                        for c in range(2):
                            P.stt("dve", cq_[:, c, :], pc[c][:, 0:qblk], gqc[:, c:c + 1], r_[:, 0:qblk], ALU.mult, ALU.mult)
                        blk[(s, q0)] = (cq_, ot_, rtq)

                def mla_A(u):
                    s, q0, h = u
                    g0 = s * L + q0
                    if (s, q0) not in blk:
                        mla_pro(s, q0)
                    if h == 2:
                        bi_ = blocks.index((s, q0))
                        if bi_ + 1 < len(blocks) and blocks[bi_ + 1] not in blk:
                            mla_pro(*blocks[bi_ + 1])
                    cq_, ot_, rtq = blk[(s, q0)]
                    hi_ = cnt["h"]
                    cnt["h"] += 1
                    q_, pt_, rc_ = qh[hi_ % 2], PT[hi_ % 2], rcp[hi_ % 2]
                    p1 = psum()
                    for c in range(2):
                        P.mm(p1[0:96, 0:qblk], wuq[:, c, h * 192:h * 192 + 96], cq_[:, c, :], start=(c == 0), stop=(c == 1))
                    P.copy("act", q_[0:64, :], p1[0:64, 0:qblk])
                    if rope:
                        p2 = psum()
                        for c in range(2):
                            P.mm(p2[0:96, 0:qblk], wuq[:, c, h * 192 + 96:h * 192 + 192], cq_[:, c, :], start=(c == 0), stop=(c == 1))
                        a_, b_ = tA[hi_ % 2], tB[hi_ % 2]
                        P.tt("dve", a_[64:96, 0:qblk], p1[64:96, 0:qblk], rtq[64:96, 0, 0:qblk], ALU.mult)
                        P.tt("dve", b_[64:96, 0:qblk], p2[64:96, 0:qblk], rtq[64:96, 1, 0:qblk], ALU.mult)
                        P.tt("pool", q_[64:96, :], a_[64:96, 0:qblk], b_[64:96, 0:qblk], ALU.add)
                    else:
                        P.copy("dve", q_[64:96, :], p1[64:96, 0:qblk])
                    uctx0[u] = (q_, pt_, rc_, ot_)

                def mla_A1(u):
                    s, q0, h = u
                    q_, pt_, rc_, ot_ = uctx0.pop(u)
                    for kt in range(0, nkt, 2):
                        pp = psum()
                        for k2 in range(2):
                            P.mm(pp[:, k2 * qblk:(k2 + 1) * qblk], khT[h][0:96, s, (kt + k2) * 128:(kt + k2 + 1) * 128], q_[0:96, :])
                        P.act(pt_[:, kt:kt + 2, :], pp.v(pp.ap[:, 0:2 * qblk].rearrange("p (a b) -> p a b", a=2)), AF.Exp, scale=float(96 ** -0.5))
                    uctx[u] = (pt_, rc_, ot_)

                def mla_B(u):
                    s, q0, h = u
                    g0 = s * L + q0
                    pt_, rc_, ot_ = uctx.pop(u)
                    pacc = psum()
                    for j in range(nqt):
                        for kt in range(nkt):
                            P.mm(pacc[:, j * 65:(j + 1) * 65], pt_[:, kt, j * 128:(j + 1) * 128], vaug[:, s * nkt + kt, h, :],
                                 start=(kt == 0), stop=(kt == nkt - 1))
                    pav = pacc.v(pacc.ap[:, 0:nqt * 65].rearrange("p (j d) -> p j d", d=65))
                    P.recip(rc_[:, 0:nqt], pav[:, :, 64])
                    for j in range(nqt):
                        P.ts("dve", ot_[:, j, h * 64:(h + 1) * 64], pav[:, j, 0:64], rc_[:, j:j + 1], ALU.mult)
                    if h == 3:
                        for j in range(nqt):
                            pp = psum()
                            ppb = pbf(pp)
                            for c in range(2):
                                P.tr(ppb[:, c * 128:(c + 1) * 128], ot_[:, j, c * 128:(c + 1) * 128], identb)
                            P.copy("act", oT[:, :, g0 + j * 128:g0 + (j + 1) * 128],
                                   ppb.v(ppb.ap[:, 0:256].rearrange("p (c t) -> p c t", c=2)))

                uctx0 = {}
                mla_A(units[0])
                if len(units) > 1:
                    mla_A(units[1])
                mla_A1(units[0])
                for ui, u in enumerate(units):
                    if ui + 2 < len(units):
                        mla_A(units[ui + 2])
                    if ui + 1 < len(units):
                        mla_A1(units[ui + 1])
                    mla_B(u)
                outproj(l, oT, 0, 2)
                if debug and "dbg_x" in debug and l == 0 and debug.get("_stage") == "mla":
                    pass

                areset()
                wC = alloc((128, 8, 896), BF16)
                wload(("wC", l), wC, I["wx"][l][:, 576:1472].rearrange("(k p) c -> p k c", p=128), ctx)
                gqt = alloc((128, 4), F32)
                P.dma("sp", gqt, I["gq"][l], writes=[gqt])
                rtabs = [alloc((128, 2, 512), F32) for _ in range(2)]
                rti = [0]

                def load_rt(src, t0, n):
                    r_ = rtabs[rti[0] % 2]
                    rti[0] += 1
                    P.dma("sp", r_[:, :, 0:n], I[src][:, :, t0:t0 + n], writes=[r_])
                    return r_
                kT = alloc((128, nseq, Lk), BF16)
                vag = alloc((128, nseq * nkt, 2, 65), BF16)
                P.memset("pool", vag[:, :, :, 64:65], 1.0)
                sqb = [alloc((128, 512), BF16) for _ in range(2)]
                rsb = [alloc((128, 512), F32) for _ in range(2)]
                tA = [alloc((128, 512), F32) for _ in range(2)]
                tB = [alloc((128, 512), F32) for _ in range(2)]
                nrm_i = [0]

                def qk_norm_rope(dst_, wcol, wcol_sw, gi, c0, n, tcol0):
                    i_ = nrm_i[0]
                    nrm_i[0] += 1
                    sq_, r_, a_, b_ = sqb[i_ % 2], rsb[i_ % 2], tA[i_ % 2], tB[i_ % 2]
                    p1 = psum()
                    for kc in range(8):
                        P.mm(p1[:, 0:n], wC[:, kc, wcol:wcol + 128], Y(kc, c0, c0 + n), start=(kc == 0), stop=(kc == 7))
                    P.act(sq_[:, 0:n], p1[:, 0:n], AF.Square)
                    pss = psum()
                    P.mm(pss[:, 0:n], bd64, sq_[:, 0:n])
                    P.act(r_[:, 0:n], pss[:, 0:n], AF.Ln, bias=EPS, scale=1.0 / 64)
                    P.act(r_[:, 0:n], r_[:, 0:n], AF.Exp, scale=-0.5)
                    if not rope:
                        P.stt("dve", dst_, p1[:, 0:n], gqt[:, gi:gi + 1], r_[:, 0:n], ALU.mult, ALU.mult)
                        return
                    p2 = psum()
                    for kc in range(8):
                        P.mm(p2[:, 0:n], wC[:, kc, wcol_sw:wcol_sw + 128], Y(kc, c0, c0 + n), start=(kc == 0), stop=(kc == 7))
                    P.stt("dve", a_[:, 0:n], p1[:, 0:n], gqt[:, gi:gi + 1], r_[:, 0:n], ALU.mult, ALU.mult)
                    P.stt("dve", b_[:, 0:n], p2[:, 0:n], gqt[:, gi + 1:gi + 2], r_[:, 0:n], ALU.mult, ALU.mult)
                    rtab = load_rt("k_ropeg", tcol0, n)
                    P.tt("pool", a_[:, 0:n], a_[:, 0:n], rtab[:, 0, 0:n], ALU.mult)
                    P.tt("pool", b_[:, 0:n], b_[:, 0:n], rtab[:, 1, 0:n], ALU.mult)
                    P.tt("dve", dst_, a_[:, 0:n], b_[:, 0:n], ALU.add)

                for s in range(nseq):
                    for k0 in range(0, L, 512):
                        n = min(512, L - k0)
                        qk_norm_rope(kT[:, s, k0:k0 + n], 512, 640, 2, s * L + k0, n, k0)
                    for kt in range(L // 128):
                        pp = psum()
                        for kc in range(8):
                            P.mm(pp[:, 0:128], Y(kc, s * L + kt * 128, s * L + (kt + 1) * 128), wC[:, kc, 768:896], start=(kc == 0), stop=(kc == 7))
                        P.copy("act", vag[:, s * nkt + kt, :, 0:64], pp.v(pp.ap[:, 0:128].rearrange("p (h d) -> p h d", h=2)))
                if ctx:
                    cst = alloc((128, 2, 128), F32)
                    P.dma("sp", cst, I["c_gk"][l].rearrange("(t p) c -> p t c", p=128), writes=[cst])
                    cv = alloc((128, 2, 128), F32)
                    P.dma("sp", cv, I["c_gv"][l].rearrange("(t p) c -> p t c", p=128), writes=[cv])
                    for t2 in range(2):
                        pp = psum()
                        P.tr(pp[:, 0:128], cst[:, t2, :], ident)
                        P.copy("act", kT[:, 0, L + t2 * 128:L + (t2 + 1) * 128], pp[:, 0:128])
                        P.copy("dve", vag[:, L // 128 + t2, :, 0:64], cv.v(cv.ap[:, t2, :].rearrange("p (h d) -> p h d", h=2)))
                oT = alloc((128, 2, T), BF16)
                qc = [[alloc((128, qblk), BF16) for _ in range(2)] for _ in range(2)]
                PT = [alloc((128, nkt, qblk), BF16) for _ in range(2)]
                otok = [alloc((128, nqt, 256), BF16) for _ in range(2)]
                rcp = [alloc((128, 4), F32) for _ in range(2)]
                units = [(s, q0, h) for s in range(nseq) for q0 in range(0, L, qblk) for h in range(4)]
                blk = {}
                uctx = {}
                cnt = {"qb": 0, "h": 0}

                blocks = [(s, q0) for s in range(nseq) for q0 in range(0, L, qblk)]

                def gqa_pro(s, q0):
                    g0 = s * L + q0
                    qbi = cnt["qb"]
                    cnt["qb"] += 1
                    qq, ot_ = qc[qbi % 2], otok[qbi % 2]
                    qk_norm_rope(qq[0], 0, 256, 0, g0, qblk, q0)
                    qk_norm_rope(qq[1], 128, 384, 0, g0, qblk, q0)
                    blk[(s, q0)] = (qq, ot_)

                def gqa_A(u):
                    s, q0, h = u
                    g0 = s * L + q0
                    if (s, q0) not in blk:
                        gqa_pro(s, q0)
                    if h == 2:
                        bi_ = blocks.index((s, q0))
                        if bi_ + 1 < len(blocks) and blocks[bi_ + 1] not in blk:
                            gqa_pro(*blocks[bi_ + 1])
                    qq, ot_ = blk[(s, q0)]
                    hi_ = cnt["h"]
                    cnt["h"] += 1
                    pt_, rc_ = PT[hi_ % 2], rcp[hi_ % 2]
                    qsrc = qq[h % 2]
                    r0 = 0 if h < 2 else 64
                    for kt in range(0, nkt, 2):
                        pp = psum()
                        for k2 in range(2):
                            P.mm(pp[:, k2 * qblk:(k2 + 1) * qblk], kT[r0:r0 + 64, s, (kt + k2) * 128:(kt + k2 + 1) * 128], qsrc[r0:r0 + 64, :])
                        P.act(pt_[:, kt:kt + 2, :], pp.v(pp.ap[:, 0:2 * qblk].rearrange("p (a b) -> p a b", a=2)), AF.Exp, scale=float(64 ** -0.5))
                    uctx[u] = (pt_, rc_, ot_)

                def gqa_B(u):
                    s, q0, h = u
                    g0 = s * L + q0
                    kvh = h // 2
                    pt_, rc_, ot_ = uctx.pop(u)
                    pacc = psum()
                    for j in range(nqt):
                        for kt in range(nkt):
                            P.mm(pacc[:, j * 65:(j + 1) * 65], pt_[:, kt, j * 128:(j + 1) * 128], vag[:, s * nkt + kt, kvh, :],
                                 start=(kt == 0), stop=(kt == nkt - 1))
                    pav = pacc.v(pacc.ap[:, 0:nqt * 65].rearrange("p (j d) -> p j d", d=65))
                    P.recip(rc_[:, 0:nqt], pav[:, :, 64])
                    for j in range(nqt):
                        P.ts("dve", ot_[:, j, h * 64:(h + 1) * 64], pav[:, j, 0:64], rc_[:, j:j + 1], ALU.mult)
                    if h == 3:
                        for j in range(nqt):
                            pp = psum()
                            ppb = pbf(pp)
                            for c in range(2):
                                P.tr(ppb[:, c * 128:(c + 1) * 128], ot_[:, j, c * 128:(c + 1) * 128], identb)
                            P.copy("act", oT[:, :, g0 + j * 128:g0 + (j + 1) * 128],
                                   ppb.v(ppb.ap[:, 0:256].rearrange("p (c t) -> p c t", c=2)))

                gqa_A(units[0])
                for ui, u in enumerate(units):
                    if ui + 1 < len(units):
                        gqa_A(units[ui + 1])
                    gqa_B(u)
                outproj(l, oT, 768, 2)

                areset()
                mlstm(l)

                areset()
                norm_mod(lambda kc: AB[:, 1, kc:kc + 1], lambda kc: mcol(l, 3, kc))
                areset()
                ffn(l)

            areset()
            P.ts("dve", AB[:, 2, :], gfin, 32.0, ALU.mult)
            sq = [alloc((128, 8, 512), BF16) for _ in range(2)]
            rs = [alloc((128, 512), F32) for _ in range(2)]
            xn = [alloc((128, 8, 512), F32) for _ in range(2)]
            ost = [alloc((128, D), F32) for _ in range(2)]
            oi = 0
            for tb in range(NB):
                c0, c1 = tb * 512, (tb + 1) * 512
                s_, r_, xn_ = sq[tb % 2], rs[tb % 2], xn[tb % 2]
                P.tt("pool", s_, xT[:, :, c0:c1], xT[:, :, c0:c1], ALU.mult)
                pss = psum()
                for kc in range(8):
                    P.mm(pss, onesb, s_[:, kc, :], start=(kc == 0), stop=(kc == 7))
                P.act(r_, pss, AF.Ln, bias=float(D * EPS))
                P.act(r_, r_, AF.Exp, scale=-0.5)
                for kc in range(8):
                    P.stt("dve", xn_[:, kc, :], X(kc, c0, c1), AB[:, 2, kc:kc + 1], r_, ALU.mult, ALU.mult)
                for j in range(4):
                    o_ = ost[oi % 2]
                    oi += 1
                    for half in range(2):
                        pp = psum()
                        for k4 in range(4):
                            kc = half * 4 + k4
                            P.tr(pp[:, k4 * 128:(k4 + 1) * 128], xn_[:, kc, j * 128:(j + 1) * 128], ident)
                        P.copy("act" if half == 0 else "dve", o_[:, half * 512:(half + 1) * 512], pp)
                    P.dma("sp", yout[c0 + j * 128:c0 + (j + 1) * 128, :], o_, reads=[o_])

            return

        mlstm = None
        ffn = None
        G = {}

        def make_stage_fns(which, nseq, L, rope, ctx):
            T = nseq * L
            NB = T // 512
            NCH = T // 128
            nch_seq = L // 128

            def X(kc, c0, c1):
                return xT[:, kc, c0:c1]

            def Y(kc, c0, c1):
                return yT[:, kc, c0:c1]

            def mcol(l, i, kc):
                return modT[:, l, i, kc, which:which + 1]

            def ffn_(l):
                SEGT = 1024
                segs = []
                if L >= SEGT:
                    for s in range(nseq):
                        for a_ in range(0, L, SEGT):
                            segs.append((s * L + a_, 1, SEGT, (s * L + a_ - 1) if a_ > 0 else None,
                                         (s * L + a_ + SEGT) if a_ + SEGT < L else None))
                else:
                    per = SEGT // L
                    for s0 in range(0, nseq, per):
                        segs.append((s0 * L, per, L, None, None))
                halves = [(0, 12), (12, 22)]
                nsub, Ls = segs[0][1], segs[0][2]
                wup = [alloc((128, 8, 2, 256), BF16) for _ in range(2)]
                wdn = [alloc((128, 12, D), BF16), alloc((128, 10, D), BF16)]
                cwt = alloc((128, 22, 3), F32)
                cbt = alloc((128, 22), F32)
                for j_ in range(3):
                    P.dma("sp", cwt[:, :, j_], I["w_ff_conv"][l, j_].rearrange("(c p) -> p c", p=128), writes=[cwt])
                P.dma("sp", cbt, I["b_ff_conv"][l].rearrange("(c p) -> p c", p=128), writes=[cbt])
                gs = [alloc((128, nsub, Ls + 2), F32) for _ in range(2)]
                for g_ in gs:
                    P.memset("pool", g_[:, :, 0:1], 0.0)
                    P.memset("pool", g_[:, :, Ls + 1:Ls + 2], 0.0)
                asb = [alloc((128, SEGT), BF16) for _ in range(2)]
                tcv = [alloc((128, nsub, Ls), F32) for _ in range(2)]
                hT = alloc((128, 12, SEGT), BF16)
                ci = 0
                tasks = [(si_, hi, pr) for si_ in range(len(segs)) for hi, (j0, j1) in enumerate(halves) for pr in range(j0 // 2, j1 // 2)]

                def load_wu(k):
                    if k >= len(tasks):
                        return
                    pr_ = tasks[k][2]
                    wu_ = wup[k % 2]
                    wload_multi([(("wu", l, pr_, half), wu_[:, :, half, :], I["w_ff_up"][l][:, half * DFF + pr_ * 256:half * DFF + pr_ * 256 + 256].rearrange("(k p) c -> p k c", p=128)) for half in range(2)], ctx)
                load_wu(0)
                tk = 0
                pending = []
                for (c0, _ns, _ls, lh, rh) in segs:
                    for hi, (j0, j1) in enumerate(halves):
                        wd = wdn[hi]
                        nj = j1 - j0
                        wload(("wd", l, hi), wd[:, 0:nj, :], I["w_ff_down"][l][j0 * 128:j1 * 128, :].rearrange("(k p) c -> p k c", p=128), ctx)
                        for pr in range(j0 // 2, j1 // 2):
                            wu = wup[tk % 2]
                            load_wu(tk + 1)
                            tk += 1
                            bufs = []
                            for cc in range(2):
                                ch = pr * 2 + cc
                                j = ch - j0
                                g_, a_, t_ = gs[ci % 2], asb[ci % 2], tcv[ci % 2]
                                ci += 1
                                bufs.append((ch, j, g_, a_, t_))
                                for tb in range(SEGT // 512):
                                    t0 = c0 + tb * 512
                                    pa, pg = psum(), psum()
                                    for kc in range(8):
                                        P.mm(pa, wu[:, kc, 0, cc * 128:(cc + 1) * 128], Y(kc, t0, t0 + 512), start=(kc == 0), stop=(kc == 7))
                                    for kc in range(8):
                                        P.mm(pg, wu[:, kc, 1, cc * 128:(cc + 1) * 128], Y(kc, t0, t0 + 512), start=(kc == 0), stop=(kc == 7))
                                    P.copy("act", a_[:, tb * 512:(tb + 1) * 512], pa)
                                    if Ls >= 512:
                                        P.copy("act", g_[:, 0, 1 + tb * 512:1 + (tb + 1) * 512], pg)
                                    else:
                                        nsq = 512 // Ls
                                        P.copy("act", g_[:, tb * nsq:(tb + 1) * nsq, 1:Ls + 1], pg.v(pg.ap.rearrange("p (a b) -> p a b", a=nsq)))
                                if L >= SEGT:
                                    for hc, col in ((lh, 0), (rh, Ls + 1)):
                                        if hc is None:
                                            P.memset("pool", g_[:, 0, col:col + 1], 0.0)
                                        else:
                                            ph = psum()
                                            for kc in range(8):
                                                P.mm(ph[:, 0:1], wu[:, kc, 1, cc * 128:(cc + 1) * 128], Y(kc, hc, hc + 1), start=(kc == 0), stop=(kc == 7))
                                            P.copy("act", g_[:, 0, col:col + 1], ph[:, 0:1])
                            if pending:
                                pending.pop()()
                            for (ch, j, g_, a_, t_) in bufs:
                                P.ts("dve", t_, g_[:, :, 1:Ls + 1], cwt[:, ch, 1:2], ALU.mult, cbt[:, ch:ch + 1], ALU.add)
                                P.stt("dve", t_, g_[:, :, 0:Ls], cwt[:, ch, 0:1], t_, ALU.mult, ALU.add)
                                P.stt("dve", t_, g_[:, :, 2:Ls + 2], cwt[:, ch, 2:3], t_, ALU.mult, ALU.add)
                                P.act(hT[:, j, :], t_.v(t_.ap.rearrange("p a b -> p (a b)")), AF.Silu)
                                P.tt("pool", hT[:, j, :], hT[:, j, :], a_, ALU.mult)

                        def down(c0=c0, wd=wd, nj=nj):
                            for tb in range(SEGT // 512):
                                t0 = c0 + tb * 512
                                for m in range(8):
                                    pp = psum()
                                    for j in range(nj):
                                        P.mm(pp, wd[:, j, m * 128:(m + 1) * 128], hT[:, j, tb * 512:(tb + 1) * 512], start=(j == 0), stop=(j == nj - 1))
                                    P.stt("dve", X(m, t0, t0 + 512), pp, mcol(l, 5, m), X(m, t0, t0 + 512), ALU.mult, ALU.add)
                        pending.append(down)
                if pending:
                    pending.pop()()

            def outproj(l, oT, row0, nkc):
                wo = alloc((128, nkc, D), BF16)
                wload(("wo", l, row0), wo, I["w_out"][l][row0:row0 + nkc * 128, :].rearrange("(k p) c -> p k c", p=128), ctx)
                for tb in range(NB):
                    c0, c1 = tb * 512, (tb + 1) * 512
                    for m in range(8):
                        pp = psum()
                        for kc in range(nkc):
                            P.mm(pp, wo[:, kc, m * 128:(m + 1) * 128], oT[:, kc, c0:c1], start=(kc == 0), stop=(kc == nkc - 1))
                        P.stt("dve", X(m, c0, c1), pp, mcol(l, 2, m), X(m, c0, c1), ALU.mult, ALU.add)

            def mlstm_(l):
                wg = alloc((128, 8, 16), BF16)
                wload(("wg", l), wg, I["wx"][l][:, 3008:3024].rearrange("(k p) c -> p k c", p=128), ctx)
                bgt = alloc((4, 4), F32)
                P.dma("sp", bgt, I["bg"][l], writes=[bgt])
                Gt = [alloc((4, T), F32) for _ in range(4)]
                for ty in range(4):
                    for tb in range(NB):
                        t0, t1 = tb * 512, (tb + 1) * 512
                        pp = psum()
                        for kc in range(8):
                            P.mm(pp[0:4, :], wg[:, kc, ty * 4:(ty + 1) * 4], Y(kc, t0, t1), start=(kc == 0), stop=(kc == 7))
                        P.act(Gt[ty][:, t0:t1], pp[0:4, :], AF.Identity, bias=bgt[:, ty:ty + 1])
                mout = alloc((4, nseq, 2), F32)
                onesT = alloc((4, L), F32)
                P.memset("pool", onesT, 1.0)
                e_ = alloc((4, T), F32)
                Bp = alloc((4, T), F32)
                a_ = alloc((4, T), F32)
                A_ = alloc((4, T), F32)
                nM_ = alloc((4, T), F32)
                nD_ = alloc((4, T), F32)
                for d in range(2):
                    li, fp = Gt[2 * d], Gt[2 * d + 1]
                    P.act(e_, fp, AF.Exp, scale=-1.0)
                    P.act(e_, e_, AF.Ln, bias=1.0)
                    for s in range(nseq):
                        sl = slice(s * L, (s + 1) * L)
                        def dirv(b_):
                            ap = b_.ap[:, sl]
                            return b_.v(ap if d == 0 else ap[:, ::-1])
                        P.scan(dirv(Bp), onesT, dirv(e_), 0.0, ALU.mult, ALU.add)
                    P.tt("dve", a_, li, Bp, ALU.add)
                    m0c = m0t[:, l * 2 + d:l * 2 + d + 1] if ctx else 0.0
                    for s in range(nseq):
                        sl = slice(s * L, (s + 1) * L)
                        def dirv(b_):
                            ap = b_.ap[:, sl]
                            return b_.v(ap if d == 0 else ap[:, ::-1])
                        P.scan(dirv(A_), dirv(a_), dirv(a_), m0c, ALU.max, ALU.max)
                    P.tt("dve", nM_, Bp, A_, ALU.subtract)
                    for s in range(nseq):
                        for c in range(nch_seq):
                            col0 = s * L + c * 128
                            first = (c == 0) if d == 0 else (c == nch_seq - 1)
                            if first:
                                prev = m0c
                            else:
                                pc_ = col0 - 1 if d == 0 else col0 + 128
                                prev = A_[:, pc_:pc_ + 1]
                            P.ts("dve", nD_[:, col0:col0 + 128], A_[:, col0:col0 + 128], prev, ALU.subtract)
                    if not ctx:
                        for s in range(nseq):
                            col = (s + 1) * L - 1 if d == 0 else s * L
                            P.ts("dve", mout[:, s, d:d + 1], nM_[:, col:col + 1], -1.0, ALU.mult)
                    for kind, tl in enumerate((A_, nD_, nM_, a_)):
                        P.dma("sp", gsc.v(gsc.ap[d, :, kind, 0:T]), tl, reads=[tl], writes=[gsc])
                if not ctx:
                    for s in range(nseq):
                        P.dma("sp", O["n_m"][s, l].rearrange("d h -> h d"), mout[:, s, :], reads=[mout])
                areset()
                mask = alloc((128, 2, 128), F32)
                P.dma("sp", mask, I["k_mask"], writes=[mask])
                cwt = alloc((128, 4, 3), F32)
                cbt = alloc((128, 4), F32)
                gml = alloc((128, 4), F32)
                for j_ in range(3):
                    P.dma("sp", cwt[:, :, j_], I["w_ml_conv"][l, j_].rearrange("(c p) -> p c", p=128), writes=[cwt])
                P.dma("sp", cbt, I["b_ml_conv"][l].rearrange("(c p) -> p c", p=128), writes=[cbt])
                P.dma("sp", gml, I["g_ml_out"][l].rearrange("(c p) -> p c", p=128), writes=[gml])
                oT = alloc((128, 4, T), BF16)
                wh2 = [alloc((128, 8, 384), BF16) for _ in range(1)]
                wqk2 = [alloc((128, 2, 128), BF16) for _ in range(2)]
                ug = alloc((128, nseq, L + 2), F32)
                tcv = alloc((128, nseq, L), F32)
                ucT = alloc((128, T), BF16)
                qT = alloc((128, T), BF16)
                kTm = alloc((128, T), BF16)
                ktok = alloc((128, NCH, 128), BF16)
                vau = alloc((128, NCH, 129), BF16)
                P.memset("pool", vau[:, :, 128:129], 1.0)
                sigo = alloc((128, T), BF16)
                hf = tcv.v(tcv.ap.rearrange("p a b -> p (a b)").rearrange("p (c e) -> p c e", e=128))
                kwb = [alloc((128, 4, 128), BF16) for _ in range(2)]
                Ugb = [alloc((128, 4, 129), F32) for _ in range(2)]
                CMb = [alloc((128, 5 if ctx else 6, 129), F32) for _ in range(3)]
                cmbb = [alloc((128, 4 if ctx else 5, 129), BF16) for _ in range(2)]
                smb = [alloc((128, 24), F32) for _ in range(2)]
                ugf = ug.ap.rearrange("p a b -> p (a b)")
                hsb = [ug.v(ugf[:, k_ * 512:(k_ + 1) * 512].rearrange("p (c e) -> p c e", e=128)) for k_ in range(2)]
                hnb = [alloc((128, 4, 128), BF16) for _ in range(2)]
                cmi = [0]
                ucf = ucT.ap.bitcast(F32)
                aw_ = min(512, T // 4)
                argc = [ucT.v(ucf[:, k_ * aw_:(k_ + 1) * aw_]) for k_ in range(2)] if ctx else [alloc((128, 512), F32) for _ in range(2)]
                wT = [alloc((128, 512), F32) for _ in range(2)]
                sT = [alloc((128, 512), BF16) for _ in range(2)]
                ie = [alloc((128, 512), F32) for _ in range(2)]
                qi = [alloc((128, 512), BF16) for _ in range(2)]
                junk = alloc((128, 128), F32)
                gls = [alloc((128, 2, 4), F32) for _ in range(2)]
                gi_c = [0]

                def head_pro(h):
                    wh, wqk = wh2[0], wqk2[h % 2]
                    wload(("wh", l, h), wh, I["wx"][l][:, 1472 + h * 384:1472 + (h + 1) * 384].rearrange("(k p) c -> p k c", p=128), ctx)
                    wload_multi([(("wq", l, h), wqk[:, 0, :], I["w_ml_q"][l, h]), (("wk", l, h), wqk[:, 1, :], I["w_ml_k"][l, h])], ctx)
                    P.memset("pool", ug[:, :, 0:1], 0.0)
                    P.memset("pool", ug[:, :, L + 1:L + 2], 0.0)
                    for tb in range(NB):
                        t0, t1 = tb * 512, (tb + 1) * 512
                        pu, po = psum(), psum()
                        for kc in range(8):
                            P.mm(pu, wh[:, kc, 0:128], Y(kc, t0, t1), start=(kc == 0), stop=(kc == 7))
                        for kc in range(8):
                            P.mm(po, wh[:, kc, 256:384], Y(kc, t0, t1), start=(kc == 0), stop=(kc == 7))
                        if L >= 512:
                            s0, o0 = t0 // L, t0 % L
                            P.copy("act", ug[:, s0, 1 + o0:1 + o0 + 512], pu)
                        else:
                            nsq = 512 // L
                            s0 = t0 // L
                            P.copy("act", ug[:, s0:s0 + nsq, 1:L + 1], pu.v(pu.ap.rearrange("p (a b) -> p a b", a=nsq)))
                        P.act(sigo[:, t0:t1], po, AF.Sigmoid)
                    P.ts("dve", tcv, ug[:, :, 1:L + 1], cwt[:, h, 1:2], ALU.mult, cbt[:, h:h + 1], ALU.add)
                    P.stt("dve", tcv, ug[:, :, 0:L], cwt[:, h, 0:1], tcv, ALU.mult, ALU.add)
                    P.stt("dve", tcv, ug[:, :, 2:L + 2], cwt[:, h, 2:3], tcv, ALU.mult, ALU.add)
                    P.act(ucT, tcv.v(tcv.ap.rearrange("p a b -> p (a b)")), AF.Silu)
                    for tb in range(NB):
                        t0, t1 = tb * 512, (tb + 1) * 512
                        pq, pk = psum(), psum()
                        P.mm(pq, wqk[:, 0, :], ucT[:, t0:t1])
                        P.mm(pk, wqk[:, 1, :], ucT[:, t0:t1])
                        P.copy("act", qT[:, t0:t1], pq)
                        P.act(kTm[:, t0:t1], pk, AF.Identity, scale=float(128 ** -0.5))
                        pkt = psum()
                        for c4 in range(4):
                            c = tb * 4 + c4
                            P.mm(pkt[:, c4 * 128:(c4 + 1) * 128], ucT[:, c * 128:(c + 1) * 128], wqk[:, 1, :])
                        P.act(ktok[:, tb * 4:tb * 4 + 4, :], pkt.v(pkt.ap.rearrange("p (a b) -> p a b", a=4)), AF.Identity, scale=float(128 ** -0.5))
                        pv = psum()
                        for c4 in range(4):
                            c = tb * 4 + c4
                            for kc in range(8):
                                P.mm(pv[:, c4 * 128:(c4 + 1) * 128], Y(kc, c * 128, (c + 1) * 128), wh[:, kc, 128:256], start=(kc == 0), stop=(kc == 7))
                        P.copy("dve", vau[:, tb * 4:tb * 4 + 4, 0:128], pv.v(pv.ap.rearrange("p (a b) -> p a b", a=4)))

                ngr = (nch_seq + 3) // 4
                units = []
                PAIR = (not ctx) and nch_seq == 2 and nseq % 2 == 0
                for h in range(4):
                    for d in range(2):
                        if PAIR:
                            for sp_ in range(nseq // 2):
                                units.append((h, d, sp_, 0, True, True, d == 0 and sp_ == 0))
                            continue
                        for s in range(nseq):
                            gorder = list(range(ngr)) if d == 0 else list(range(ngr - 1, -1, -1))
                            for gi2, gq_ in enumerate(gorder):
                                units.append((h, d, s, gq_, gi2 == 0, gi2 == ngr - 1, d == 0 and s == 0 and gi2 == 0))
                uctx = {}
                uctx1 = {}
                uctx2 = {}

                def unit_A(u):
                    h, d, s, gq_, first_sd, last_sd, first_h = u
                    if first_h:
                        head_pro(h)
                    gi_ = gi_c[0]
                    cg0 = gq_ * 4
                    ng = min(4, nch_seq - cg0)
                    col0 = s * L + cg0 * 128
                    if PAIR:
                        ng = 4
                        col0 = s * 512
                    W = ng * 128
                    ac, w_, s_, ie_, qi__ = argc[gi_ % 2], wT[gi_ % 2], sT[gi_ % 2], ie[gi_ % 2], qi[gi_ % 2]
                    gl = gls[gi_ % 2]
                    gi_c[0] += 1
                    gi_ = gi_c[0]
                    P.dma("sp", ac[:, 0:W], gsc.v(gsc.ap[d, h, 0, col0:col0 + W].partition_broadcast(128)), reads=[gsc], writes=[ac])
                    P.dma("sp", ie_[:, 0:W], gsc.v(gsc.ap[d, h, 1, col0:col0 + W].partition_broadcast(128)), reads=[gsc], writes=[ie_])
                    P.dma("sp", gl[:, 0, 0:ng], gsc.v(gsc.ap[d, h, 3, col0:col0 + W].rearrange("(i p) -> p i", p=128)), reads=[gsc], writes=[gl])
                    P.dma("sp", gl[:, 1, 0:ng], gsc.v(gsc.ap[d, h, 2, col0:col0 + W].rearrange("(i p) -> p i", p=128)), reads=[gsc], writes=[gl])
                    for i in range(ng):
                        P.stt("dve", w_[:, i * 128:(i + 1) * 128], ac[:, i * 128:(i + 1) * 128], gl[:, 0, i:i + 1], mask[:, d, :], ALU.subtract, ALU.max)
                    P.act(w_[:, 0:W], w_[:, 0:W], AF.Exp, scale=-1.0)
                    pst = psum()
                    for i in range(ng):
                        cc0 = col0 + i * 128
                        P.mm(pst[:, i * 128:(i + 1) * 128], kTm[:, cc0:cc0 + 128], qT[:, cc0:cc0 + 128])
                    P.tt("dve", s_[:, 0:W], pst[:, 0:W], w_[:, 0:W], ALU.mult)
                    P.act(ie_[:, 0:W], ie_[:, 0:W], AF.Exp, scale=-1.0)
                    P.tt("pool", qi__[:, 0:W], qT[:, col0:col0 + W], ie_[:, 0:W], ALU.mult)
                    corder = list(range(ng)) if d == 0 else list(range(ng - 1, -1, -1))
                    ecs = [(i * 128 + 127) if d == 0 else (i * 128) for i in range(ng)]
                    cs_ = [s * nch_seq + cg0 + i for i in range(ng)]
                    if PAIR:
                        corder = [0, 1, 2, 3] if d == 0 else [1, 0, 3, 2]
                        cs_ = [s * 4 + i for i in range(4)]
                    n3 = min(ng, 3)
                    kwg, Ug, cmbg, sm_ = kwb[gi_ % 2], Ugb[gi_ % 2], cmbb[gi_ % 2], smb[gi_ % 2]
                    hs4, hn4 = hsb[gi_ % 2], hnb[gi_ % 2]
                    P.act(sm_[:, 0:ng], gl[:, 1, 0:ng], AF.Exp)
                    uctx1[u] = (ng, w_, ecs, cs_, n3, kwg, Ug)
                    uctx[u] = (cg0, ng, col0, W, gl, w_, s_, ie_, qi__, corder, ecs, cs_, n3, Ug, cmbg, sm_, hs4, hn4)

                def unit_A2(u):
                    (ng, w_, ecs, cs_, n3, kwg, Ug) = uctx1.pop(u)
                    e0_ = ecs[0]
                    wkb = w_.v(w_.ap[:, 0:ng * 128].rearrange("p (a b) -> p a b", b=128)[:, :, e0_:e0_ + 1].broadcast_to([128, ng, 128]))
                    P.tt("dve", kwg[:, 0:ng, :], ktok[:, cs_[0]:cs_[0] + ng, :], wkb, ALU.mult)
                    pUa = psum()
                    pUb = psum() if ng == 4 else None
                    for i in range(ng):
                        dst_ = pUa[:, i * 129:(i + 1) * 129] if i < 3 else pUb[:, 0:129]
                        P.mm(dst_, kwg[:, i, :], vau[:, cs_[i], :])
                    P.copy("act", Ug[:, 0:n3, :], pUa.v(pUa.ap[:, 0:n3 * 129].rearrange("p (a b) -> p a b", a=n3)))
                    if ng == 4:
                        P.copy("act", Ug[:, 3, :], pUb[:, 0:129])

                def unit_B(u):
                    h, d, s, gq_, first_sd, last_sd, first_h = u
                    (cg0, ng, col0, W, gl, w_, s_, ie_, qi__, corder, ecs, cs_, n3, Ug, cmbg, sm_, hs4, hn4) = uctx.pop(u)
                    if first_sd:
                        cm0 = CMb[cmi[0] % 3]
                        if ctx:
                            P.dma("sp", cm0[:, 0, 0:128], I["C0"][l, d, h], writes=[cm0])
                            P.dma("sp", cm0[:, 0, 128:129], I["n0"][l, d, h].rearrange("(p o) -> p o", o=1), writes=[cm0])
                        else:
                            P.memset("pool", cm0[:, 0, :], 0.0)
                            if PAIR:
                                P.memset("pool", cm0[:, 3, :], 0.0)
                    CMg = CMb[cmi[0] % 3]
                    CMn = CMb[(cmi[0] + 1) % 3]
                    cmi[0] += 1
                    if PAIR:
                        slot = [0, 1, 3, 4]
                        for k, i in enumerate(corder):
                            P.stt("dve", CMg[:, slot[k] + 1, :], CMg[:, slot[k], :], ie_[:, ecs[i]:ecs[i] + 1], Ug[:, i, :], ALU.mult, ALU.add)
                        P.copy("act", cmbg[:, 0:5, :], CMg[:, 0:5, :])
                    else:
                        slot = list(range(ng))
                        for k, i in enumerate(corder):
                            out_ = CMg[:, k + 1, :] if k < ng - 1 else CMn[:, 0, :]
                            P.stt("dve", out_, CMg[:, k, :], ie_[:, ecs[i]:ecs[i] + 1], Ug[:, i, :], ALU.mult, ALU.add)
                        P.copy("act", cmbg[:, 0:ng, :], CMg[:, 0:ng, :])
                    pnA = psum()
                    pnB = psum()
                    for k, i in enumerate(corder):
                        dst_ = pnA[:, i * 129:(i + 1) * 129] if i < 3 else pnB[:, 0:129]
                        P.mm(dst_, s_[:, i * 128:(i + 1) * 128], vau[:, cs_[i], :], start=True, stop=False)
                        P.mm(dst_, qi__[:, i * 128:(i + 1) * 128], cmbg[:, slot[k], :], start=False, stop=True)
                    uctx2[u] = (cg0, ng, col0, W, cs_, n3, sm_, hs4, hn4, pnA, pnB, CMg, gl)

                def unit_B2(u):
                    h, d, s, gq_, first_sd, last_sd, first_h = u
                    (cg0, ng, col0, W, cs_, n3, sm_, hs4, hn4, pnA, pnB, CMg_, gl_) = uctx2.pop(u)
                    denA = pnA.v(pnA.ap[:, 0:n3 * 129].rearrange("p (a b) -> p a b", b=129)[:, :, 128])
                    P.ts("dve", sm_[:, 4:4 + n3], denA, -1.0, ALU.mult)
                    P.tt("dve", sm_[:, 4:4 + n3], sm_[:, 4:4 + n3], denA, ALU.max)
                    if ng == 4:
                        P.ts("dve", sm_[:, 7:8], pnB[:, 128:129], -1.0, ALU.mult)
                        P.tt("dve", sm_[:, 7:8], sm_[:, 7:8], pnB[:, 128:129], ALU.max)
                    P.tt("dve", sm_[:, 4:4 + ng], sm_[:, 4:4 + ng], sm_[:, 0:ng], ALU.max)
                    P.recip(sm_[:, 8:8 + ng], sm_[:, 4:4 + ng])
                    if d == 0:
                        rcb = sm_.v(sm_.ap[:, 8:8 + n3].unsqueeze(2).broadcast_to([128, n3, 128]))
                        P.tt("dve", hf[:, cs_[0]:cs_[0] + n3, :], pnA.v(pnA.ap[:, 0:n3 * 129].rearrange("p (a b) -> p a b", b=129)[:, :, 0:128]), rcb, ALU.mult)
                        if ng == 4:
                            P.ts("dve", hf[:, cs_[3], :], pnB[:, 0:128], sm_[:, 11:12], ALU.mult)
                    for i in range(ng):
                        src_ = pnA[:, i * 129:i * 129 + 128] if i < 3 else pnB[:, 0:128]
                        if d == 0:
                            pass
                        else:
                            P.stt("dve", hs4[:, i, :], src_, sm_[:, 8 + i:9 + i], hf[:, cs_[i], :], ALU.mult, ALU.add)
                            P.act(junk, hs4[:, i, :], AF.Square, accum=sm_[:, 12 + i:13 + i])
                    if d == 1:
                        P.act(sm_[:, 16:16 + ng], sm_[:, 12:12 + ng], AF.Ln, bias=EPS, scale=1.0 / 128)
                        P.act(sm_[:, 20:20 + ng], sm_[:, 16:16 + ng], AF.Exp, scale=-0.5)

                        def b3(h=h, ng=ng, hn4=hn4, hs4=hs4, sm_=sm_, col0=col0, W=W):
                            for i in range(ng):
                                P.ts("dve", hn4[:, i, :], hs4[:, i, :], sm_[:, 20 + i:21 + i], ALU.mult)
                            ptr = psum()
                            ptb = pbf(ptr)
                            for i in range(ng):
                                P.tr(ptb[:, i * 128:(i + 1) * 128], hn4[:, i, :], identb)
                            P.stt("dve", oT[:, h, col0:col0 + W], ptb[:, 0:W], gml[:, h:h + 1], sigo[:, col0:col0 + W], ALU.mult, ALU.mult)
                        pend3.append(b3)
                    if last_sd:
                        cmfin = CMb[cmi[0] % 3]
                        if PAIR:
                            for sq_, sl_ in ((2 * s, 2), (2 * s + 1, 5)):
                                P.dma("sp", O["n_C"][sq_, l, d, h], CMg_[:, sl_, 0:128], reads=[CMg_])
                                P.dma("sp", O["n_n"][sq_, l, d, h].rearrange("(p o) -> p o", o=1), CMg_[:, sl_, 128:129], reads=[CMg_])
                        elif not ctx:
                            P.dma("sp", O["n_C"][s, l, d, h], cmfin[:, 0, 0:128], reads=[cmfin])
                            P.dma("sp", O["n_n"][s, l, d, h].rearrange("(p o) -> p o", o=1), cmfin[:, 0, 128:129], reads=[cmfin])

                pend3 = []
                unit_A(units[0])
                unit_A2(units[0])
                for ui, u in enumerate(units):
                    nxt = units[ui + 1] if ui + 1 < len(units) else None
                    pipel = nxt is not None and not nxt[6]
                    unit_B(u)
                    if pipel:
                        unit_A(nxt)
                        unit_A2(nxt)
                    while pend3:
                        pend3.pop(0)()
                    unit_B2(u)
                    if nxt is None or nxt[6]:
                        while pend3:
                            pend3.pop(0)()
                    if nxt is not None and nxt[6]:
                        unit_A(nxt)
                        unit_A2(nxt)
                outproj(l, oT, 256, 4)

            return mlstm_, ffn_

        if do_p:
            mlstm, ffn = make_stage_fns(0, 4, 256, False, False)
            run_group("P", 0, I["xp"], O["yp"], 4, 256, False, False)
        if do_s:
            mlstm, ffn = make_stage_fns(1, 1, 2048, True, True)
            run_group("S", 1, I["xs"], O["ys"], 1, 2048, True, True)
        P.barrier()
        P.emit()
    return nc, P


_CACHE = {}


def kernel(x_prompt, x_sample, cache_mla_ckv, cache_mla_krope, cache_gqa_k, cache_gqa_v,
           state_mlstm_C, state_mlstm_n, state_mlstm_m, c, c_ctx,
           w_ada, b_ada, g_norm1, g_norm2, w_in, g_mla_q, w_mla_uq, g_mla_kv, w_mla_ukv,
           w_ml_conv, b_ml_conv, w_ml_q, w_ml_k, b_ml_gates, g_ml_out, g_gqa_q, g_gqa_k,
           w_out, w_ff_up, w_ff_conv, b_ff_conv, w_ff_down, g_final, _do_p=True, _do_s=True, _cores=8):
    f = lambda a: np.ascontiguousarray(np.asarray(a, dtype=np.float32))
    wx, uqx, ukvx, gq, bg = _layout_weights(f(w_in), f(w_mla_uq), f(w_mla_ukv), f(g_gqa_q), f(g_gqa_k), f(b_ml_gates))
    consts = _consts()
    shared = {
        "w_ada": f(w_ada), "b_ada": f(b_ada), "g_norm1": f(g_norm1), "g_norm2": f(g_norm2), "wx": wx,
        "g_mla_q": f(g_mla_q), "uqx": uqx, "g_mla_kv": f(g_mla_kv), "ukvx": ukvx,
        "w_ml_conv": f(w_ml_conv), "b_ml_conv": f(b_ml_conv), "w_ml_q": f(w_ml_q), "w_ml_k": f(w_ml_k),
        "bg": bg, "g_ml_out": f(g_ml_out), "gq": gq, "g_gqa_k": f(g_gqa_k),
        "w_out": f(w_out), "w_ff_up": f(w_ff_up), "w_ff_conv": f(w_ff_conv), "b_ff_conv": f(b_ff_conv),
        "w_ff_down": f(w_ff_down), "g_final": f(g_final),
    }
    shared.update(consts)
    xp, xs = f(x_prompt), f(x_sample)
    in_maps = []
    for i in range(_cores):
        m = dict(shared)
        m["xp"] = xp[4 * i:4 * i + 4].reshape(1024, D)
        m["xs"] = xs[i]
        m["c_ckv"] = f(cache_mla_ckv)[i]
        m["c_kr"] = f(cache_mla_krope)[i]
        m["c_gk"] = f(cache_gqa_k)[i].reshape(DEPTH, 256, 128)
        m["c_gv"] = f(cache_gqa_v)[i].reshape(DEPTH, 256, 128)
        m["C0"] = f(state_mlstm_C)[i]
        m["n0"] = f(state_mlstm_n)[i]
        m["m0"] = f(state_mlstm_m)[i]
        m["cvec"] = np.stack([f(c_ctx), f(c)[i]], 0)
        in_maps.append(m)
    key = (_do_p, _do_s)
    if key not in _CACHE:
        _CACHE[key] = build_program(_do_p, _do_s)[0]
    nc = _CACHE[key]
    res = run_bass_kernel_spmd(nc, in_maps, core_ids=list(range(_cores)))
    R = res.results
    cat = lambda k: np.concatenate([r[k] for r in R], 0)
    y_prompt = cat("yp").reshape(-1, 256, D)
    y_sample = np.stack([r["ys"] for r in R], 0)
    n_ckv = cat("n_ckv")
    n_kr = cat("n_kr")
    n_k = cat("n_k").reshape(-1, DEPTH, 256, 2, 64)
    n_v = cat("n_v").reshape(-1, DEPTH, 256, 2, 64)
    return (y_prompt, y_sample, n_ckv, n_kr, n_k, n_v, cat("n_C"), cat("n_n"), cat("n_m"))
```

```python
import math
from contextlib import ExitStack
import numpy as np
import concourse.bass as bass
import concourse.mybir as mybir
from concourse.bass_utils import run_bass_kernel_spmd

F32 = mybir.dt.float32
BF16 = mybir.dt.bfloat16
AF = mybir.ActivationFunctionType
ALU = mybir.AluOpType
AX = mybir.AxisListType

ENGS = ("pe", "act", "dve", "pool", "sp")
NDMA = 24
SEM_ROT = 30000
EPS = 1e-6
DEPTH = 2
D = 1024
DFF = 2816
NX = 3456
DEBUG_ARENA = False


class Dep:
    __slots__ = ("w", "r")

    def __init__(self):
        self.w = None
        self.r = {}


class Buf:
    __slots__ = ("ap", "dep")

    def __init__(self, ap, dep=None):
        self.ap = ap
        self.dep = dep if dep is not None else Dep()

    def __getitem__(self, k):
        return Buf(self.ap[k], self.dep)

    def v(self, ap):
        return Buf(ap, self.dep)


def _ap(x):
    return x.ap if isinstance(x, Buf) else x


class Prog:
    def __init__(self, nc):
        self.nc = nc
        self.ops = {e: [] for e in ENGS}
        self.cnt = {e: 0 for e in ENGS}
        self.epoch = {e: 0 for e in ENGS}
        self.known = {e: {} for e in ENGS}
        self.dma_val = [0] * NDMA
        self.dma_next = 0
        self.dma_next_sw = 0
        self.n_ins = 0

    def _need(self, eng, deps):
        kn = self.known[eng]
        for sk, v in deps.items():
            if kn.get(sk, 0) >= v:
                continue
            if sk[0] == eng and eng == "pe":
                continue
            kn[sk] = v
            self.ops[eng].append(("wait", sk, v))

    @staticmethod
    def _add(deps, tok):
        if tok is None:
            return
        sk, v = tok
        if deps.get(sk, 0) < v:
            deps[sk] = v

    def _collect(self, reads, writes):
        deps = {}
        for b in reads:
            self._add(deps, b.dep.w)
        for b in writes:
            self._add(deps, b.dep.w)
            for sk, v in b.dep.r.items():
                self._add(deps, (sk, v))
        return deps

    def _record(self, tok, reads, writes):
        sk, v = tok
        for b in reads:
            if b.dep.r.get(sk, 0) < v:
                b.dep.r[sk] = v
        for b in writes:
            b.dep.w = tok
            b.dep.r = {}

    def op(self, eng, fn, reads=(), writes=()):
        reads = [b for b in reads if isinstance(b, Buf)]
        writes = [b for b in writes if isinstance(b, Buf)]
        deps = self._collect(reads, writes)
        self._need(eng, deps)
        if self.cnt[eng] >= SEM_ROT:
            self.epoch[eng] += 1
            self.cnt[eng] = 0
        sk = (eng, self.epoch[eng])
        self.cnt[eng] += 1
        v = self.cnt[eng]
        self.ops[eng].append(("ins", fn, sk, 1))
        if eng == "pe":
            self.known[eng][sk] = v
        self._record((sk, v), reads, writes)
        self.n_ins += 1

    def dma(self, q, out, in_, reads=(), writes=()):
        reads = [b for b in reads if isinstance(b, Buf)]
        writes = [b for b in writes if isinstance(b, Buf)]
        deps = self._collect(reads, writes)
        half = NDMA // 2
        if q == "pool":
            j = half + self.dma_next_sw
            self.dma_next_sw = (self.dma_next_sw + 1) % half
        else:
            j = self.dma_next
            self.dma_next = (j + 1) % half
        sk = ("dma", j)
        if self.dma_val[j] > 0:
            self._add(deps, (sk, self.dma_val[j]))
        self._need(q, deps)
        self.dma_val[j] += 16
        v = self.dma_val[j]
        o, i = _ap(out), _ap(in_)
        def _issue(e, o=o, i=i):
            try:
                return e.dma_start(out=o, in_=i)
            except ValueError:
                return e.dma_start(out=o, in_=i, allow_slow_non_contiguous=True)
        self.ops[q].append(("ins", _issue, sk, 16))
        self._record((sk, v), reads, writes)
        self.n_ins += 1

    def barrier(self):
        deps = {}
        for e in ENGS:
            for ep in range(self.epoch[e] + 1):
                val = self.cnt[e] if ep == self.epoch[e] else SEM_ROT
                if val > 0:
                    deps[(e, ep)] = val
        for j in range(NDMA):
            if self.dma_val[j] > 0:
                deps[("dma", j)] = self.dma_val[j]
        for e in ENGS:
            kn = self.known[e]
            for sk, v in deps.items():
                if sk[0] == e and e == "pe":
                    continue
                if kn.get(sk, 0) >= v:
                    continue
                kn[sk] = v
                self.ops[e].append(("wait", sk, v))

    def emit(self):
        nc = self.nc
        semkeys = set()
        for e in ENGS:
            for it in self.ops[e]:
                semkeys.add(it[1] if it[0] == "wait" else it[2])
        semkeys = sorted(semkeys, key=str)
        with ExitStack() as st:
            sems = {}
            for sk in semkeys:
                sems[sk] = st.enter_context(nc.semaphore(f"s_{sk[0]}_{sk[1]}"))
            block = st.enter_context(nc.Block())
            ops = self.ops
            needed = {}
            for e in ENGS:
                for it in ops[e]:
                    if it[0] == "wait" and it[1][0] != "dma":
                        needed.setdefault(it[1], set()).add(it[2])
            remap = {}
            for e in ENGS:
                idx = {}
                cnt = {}
                for it in ops[e]:
                    if it[0] == "ins" and it[2][0] != "dma":
                        sk = it[2]
                        idx[sk] = idx.get(sk, 0) + 1
                        if idx[sk] in needed.get(sk, ()):
                            cnt[sk] = cnt.get(sk, 0) + 1
                            remap[(sk, idx[sk])] = cnt[sk]
            self.n_inc = len(remap)

            def replay(engine, lst):
                idx = {}
                for it in lst:
                    if it[0] == "wait":
                        if it[1][0] == "dma":
                            engine.wait_ge(sems[it[1]], it[2])
                        else:
                            engine.wait_ge(sems[it[1]], remap[(it[1], it[2])])
                    else:
                        sk = it[2]
                        if sk[0] == "dma":
                            it[1](engine).then_inc(sems[sk], it[3])
                        else:
                            idx[sk] = idx.get(sk, 0) + 1
                            ins = it[1](engine)
                            if (sk, idx[sk]) in remap:
                                ins.then_inc(sems[sk], 1)

            @block.tensor
            def _(e):
                replay(e, ops["pe"])

            @block.scalar
            def _(e):
                replay(e, ops["act"])

            @block.vector
            def _(e):
                replay(e, ops["dve"])

            @block.gpsimd
            def _(e):
                replay(e, ops["pool"])

            @block.sync
            def _(e):
                replay(e, ops["sp"])

    def mm(self, out, lhsT, rhs, start=True, stop=True):
        self.op("pe", lambda e: e.matmul(out.ap, lhsT.ap, rhs.ap, start=start, stop=stop),
                reads=[lhsT, rhs], writes=[out])

    def tr(self, out, in_, ident):
        self.op("pe", lambda e: e.transpose(out.ap, in_.ap, ident.ap), reads=[in_, ident], writes=[out])

    def act(self, out, in_, func, bias=0.0, scale=1.0, accum=None):
        kw = {}
        if accum is not None:
            kw["accum_out"] = accum.ap
        b, s = _ap(bias), _ap(scale)
        self.op("act", lambda e: e.activation(out.ap, in_.ap, func, bias=b, scale=s, **kw),
                reads=[in_, bias, scale], writes=[out] + ([accum] if accum is not None else []))

    def tt(self, eng, out, a, b, op):
        self.op(eng, lambda e: e.tensor_tensor(out.ap, a.ap, b.ap, op), reads=[a, b], writes=[out])

    def ts(self, eng, out, a, s1, op0, s2=None, op1=None):
        x1, x2 = _ap(s1), _ap(s2)
        if op1 is None:
            self.op(eng, lambda e: e.tensor_scalar(out.ap, a.ap, x1, None, op0), reads=[a, s1], writes=[out])
        else:
            self.op(eng, lambda e: e.tensor_scalar(out.ap, a.ap, x1, x2, op0, op1), reads=[a, s1, s2], writes=[out])

    def stt(self, eng, out, in0, scalar, in1, op0, op1):
        sc = _ap(scalar)
        self.op(eng, lambda e: e.scalar_tensor_tensor(out.ap, in0.ap, sc, in1.ap, op0, op1),
                reads=[in0, scalar, in1], writes=[out])

    def copy(self, eng, out, in_):
        if eng == "act":
            self.op("act", lambda e: e.activation(out.ap, in_.ap, AF.Copy), reads=[in_], writes=[out])
        else:
            self.op(eng, lambda e: e.tensor_copy(out.ap, in_.ap), reads=[in_], writes=[out])

    def memset(self, eng, out, val):
        self.op(eng, lambda e: e.memset(out.ap, val), writes=[out])

    def recip(self, out, in_):
        self.op("dve", lambda e: e.reciprocal(out.ap, in_.ap), reads=[in_], writes=[out])

    def scan(self, out, d0, d1, init, op0, op1):
        iv = _ap(init)
        self.op("dve", lambda e: e.tensor_tensor_scan(out.ap, d0.ap, d1.ap, iv, op0, op1),
                reads=[d0, d1, init], writes=[out])


def _rope_tables():
    T = 2048
    t = np.arange(T)
    rows = (t // 64).astype(np.float32)
    cols = (t % 64).astype(np.float32)

    def tab(d):
        h = d // 2
        q = h // 2
        inv = (10000.0 ** (-np.arange(0, h, 2, dtype=np.float32) / h)).astype(np.float32)
        C = np.zeros((d, T), np.float32)
        S = np.zeros((d, T), np.float32)
        for g, pos in enumerate((rows, cols)):
            ang = pos[None, :] * inv[:, None]
            c, s = np.cos(ang).astype(np.float32), np.sin(ang).astype(np.float32)
            C[g * h:g * h + q] = c
            C[g * h + q:g * h + h] = c
            S[g * h:g * h + q] = -s
            S[g * h + q:g * h + h] = s
        return C, S

    Cm, Sm = tab(32)
    Cg, Sg = tab(64)
    rm = np.zeros((128, 2, T), np.float32)
    rm[64:96, 0] = Cm
    rm[64:96, 1] = Sm
    rg = np.zeros((128, 2, T), np.float32)
    rg[0:64, 0] = Cg
    rg[64:128, 0] = Cg
    rg[0:64, 1] = Sg
    rg[64:128, 1] = Sg
    return rm, rg


def _swap_idx(d):
    h = d // 2
    q = h // 2
    idx = np.arange(d)
    out = idx.copy()
    for g in range(2):
        out[g * h:g * h + q] = idx[g * h + q:g * h + h]
        out[g * h + q:g * h + h] = idx[g * h:g * h + q]
    return out


def _layout_weights(w_in, w_mla_uq, w_mla_ukv, g_gqa_q, g_gqa_k, b_ml_gates):
    L = w_in.shape[0]
    o_cq, o_ckv, o_kr, o_u, o_v, o_o, o_g, o_qg, o_kg, o_vg = 0, 256, 384, 416, 928, 1440, 1952, 1968, 2224, 2352
    sw32 = _swap_idx(32)
    sw64 = _swap_idx(64)
    wx = np.zeros((L, D, NX), np.float32)
    wx[:, :, 0:256] = w_in[:, :, o_cq:o_cq + 256]
    wx[:, :, 256:384] = w_in[:, :, o_ckv:o_ckv + 128]
    wx[:, :, 384 + 64:480] = w_in[:, :, o_kr:o_kr + 32]
    wx[:, :, 480 + 64:576] = w_in[:, :, o_kr + sw32]
    qcols = lambda h: np.arange(o_qg + h * 64, o_qg + (h + 1) * 64)
    qA = np.concatenate([qcols(0), qcols(2)])
    qB = np.concatenate([qcols(1), qcols(3)])
    qAs = np.concatenate([qcols(0)[sw64], qcols(2)[sw64]])
    qBs = np.concatenate([qcols(1)[sw64], qcols(3)[sw64]])
    wx[:, :, 576:704] = w_in[:, :, qA]
    wx[:, :, 704:832] = w_in[:, :, qB]
    wx[:, :, 832:960] = w_in[:, :, qAs]
    wx[:, :, 960:1088] = w_in[:, :, qBs]
    kc = np.arange(o_kg, o_kg + 128)
    kcs = np.concatenate([kc[0:64][sw64], kc[64:128][sw64]])
    wx[:, :, 1088:1216] = w_in[:, :, kc]
    wx[:, :, 1216:1344] = w_in[:, :, kcs]
    wx[:, :, 1344:1472] = w_in[:, :, o_vg:o_vg + 128]
    for h in range(4):
        b = 1472 + h * 384
        wx[:, :, b:b + 128] = w_in[:, :, o_u + h * 128:o_u + (h + 1) * 128]
        wx[:, :, b + 128:b + 256] = w_in[:, :, o_v + h * 128:o_v + (h + 1) * 128]
        wx[:, :, b + 256:b + 384] = w_in[:, :, o_o + h * 128:o_o + (h + 1) * 128]
    wx[:, :, 3008:3024] = w_in[:, :, o_g:o_g + 16]
    wx[:, :, 3040:3168] = w_in[:, :, o_ckv:o_ckv + 128]
    wx[:, :, 3168:3200] = w_in[:, :, o_kr:o_kr + 32]
    wx[:, :, 3200:3328] = w_in[:, :, o_kg:o_kg + 128]
    wx[:, :, 3328:3456] = w_in[:, :, o_vg:o_vg + 128]
    uqx = np.zeros((L, 256, 4, 192), np.float32)
    for h in range(4):
        uqx[:, :, h, 0:96] = w_mla_uq[:, :, h * 96:(h + 1) * 96]
        uqx[:, :, h, 96 + 64:192] = w_mla_uq[:, :, h * 96 + 64 + sw32]
    uqx = uqx.reshape(L, 256, 768)
    ukvx = np.zeros((L, 128, 512), np.float32)
    for h in range(4):
        ukvx[:, :, h * 64:(h + 1) * 64] = w_mla_ukv[:, :, h * 128:h * 128 + 64]
        ukvx[:, :, 256 + h * 64:256 + (h + 1) * 64] = w_mla_ukv[:, :, h * 128 + 64:(h + 1) * 128]
    gq = np.stack([np.concatenate([g_gqa_q, g_gqa_q], 1), np.concatenate([g_gqa_q[:, sw64], g_gqa_q[:, sw64]], 1),
                   np.concatenate([g_gqa_k, g_gqa_k], 1), np.concatenate([g_gqa_k[:, sw64], g_gqa_k[:, sw64]], 1)], 2)
    bg = np.ascontiguousarray(b_ml_gates.reshape(L, 4, 4).transpose(0, 2, 1))
    return wx, uqx, ukvx, np.ascontiguousarray(gq.astype(np.float32)), bg.astype(np.float32)


def _consts():
    c = {}
    c["k_ident"] = np.eye(128, dtype=np.float32)
    sel = np.zeros((4, 4, 128), np.float32)
    for h in range(4):
        sel[h, h, :] = 1.0
    c["k_sel"] = sel
    c["k_nsel"] = -sel
    c["k_noh"] = (-np.eye(4)).astype(np.float32)
    c["k_oh"] = np.eye(4, dtype=np.float32)
    s = np.arange(128)[:, None]
    t = np.arange(128)[None, :]
    mf = np.where(s <= t, 0.0, 1e4).astype(np.float32)
    mb = np.where(s >= t, 0.0, 1e4).astype(np.float32)
    c["k_mask"] = np.stack([mf, mb], 1)
    rm, rg = _rope_tables()
    c["k_ropem"] = rm
    c["k_ropeg"] = rg
    return c


def build_program(do_p=True, do_s=True, debug=None):
    nc = bass.Bass("TRN2", target_bir_lowering=False)
    P = Prog(nc)

    def din(name, shape):
        return nc.dram_tensor(name, list(shape), F32, kind="ExternalInput").ap()

    def dout(name, shape):
        return nc.dram_tensor(name, list(shape), F32, kind="ExternalOutput").ap()

    I = {}
    I["xp"] = din("xp", (1024, D))
    I["xs"] = din("xs", (2048, D))
    I["c_ckv"] = din("c_ckv", (DEPTH, 256, 128))
    I["c_kr"] = din("c_kr", (DEPTH, 256, 32))
    I["c_gk"] = din("c_gk", (DEPTH, 256, 128))
    I["c_gv"] = din("c_gv", (DEPTH, 256, 128))
    I["C0"] = din("C0", (DEPTH, 2, 4, 128, 128))
    I["n0"] = din("n0", (DEPTH, 2, 4, 128))
    I["m0"] = din("m0", (DEPTH, 2, 4))
    I["cvec"] = din("cvec", (2, D))
    I["w_ada"] = din("w_ada", (DEPTH, D, 6 * D))
    I["b_ada"] = din("b_ada", (DEPTH, 6 * D))
    I["g_norm1"] = din("g_norm1", (DEPTH, D))
    I["g_norm2"] = din("g_norm2", (DEPTH, D))
    I["wx"] = din("wx", (DEPTH, D, NX))
    I["g_mla_q"] = din("g_mla_q", (DEPTH, 256))
    I["uqx"] = din("uqx", (DEPTH, 256, 768))
    I["g_mla_kv"] = din("g_mla_kv", (DEPTH, 128))
    I["ukvx"] = din("ukvx", (DEPTH, 128, 512))
    I["w_ml_conv"] = din("w_ml_conv", (DEPTH, 3, 512))
    I["b_ml_conv"] = din("b_ml_conv", (DEPTH, 512))
    I["w_ml_q"] = din("w_ml_q", (DEPTH, 4, 128, 128))
    I["w_ml_k"] = din("w_ml_k", (DEPTH, 4, 128, 128))
    I["bg"] = din("bg", (DEPTH, 4, 4))
    I["g_ml_out"] = din("g_ml_out", (DEPTH, 512))
    I["gq"] = din("gq", (DEPTH, 128, 4))
    I["g_gqa_k"] = din("g_gqa_k", (DEPTH, 64))
    I["w_out"] = din("w_out", (DEPTH, D, D))
    I["w_ff_up"] = din("w_ff_up", (DEPTH, D, 2 * DFF))
    I["w_ff_conv"] = din("w_ff_conv", (DEPTH, 3, DFF))
    I["b_ff_conv"] = din("b_ff_conv", (DEPTH, DFF))
    I["w_ff_down"] = din("w_ff_down", (DEPTH, DFF, D))
    I["g_final"] = din("g_final", (D,))
    I["k_ident"] = din("k_ident", (128, 128))
    I["k_sel"] = din("k_sel", (4, 4, 128))
    I["k_noh"] = din("k_noh", (4, 4))
    I["k_nsel"] = din("k_nsel", (4, 4, 128))
    I["k_oh"] = din("k_oh", (4, 4))
    I["k_mask"] = din("k_mask", (128, 2, 128))
    I["k_ropem"] = din("k_ropem", (128, 2, 2048))
    I["k_ropeg"] = din("k_ropeg", (128, 2, 2048))
    O = {}
    O["yp"] = dout("yp", (1024, D))
    O["ys"] = dout("ys", (2048, D))
    O["n_ckv"] = dout("n_ckv", (4, DEPTH, 256, 128))
    O["n_kr"] = dout("n_kr", (4, DEPTH, 256, 32))
    O["n_k"] = dout("n_k", (4, DEPTH, 256, 128))
    O["n_v"] = dout("n_v", (4, DEPTH, 256, 128))
    O["n_C"] = dout("n_C", (4, DEPTH, 2, 4, 128, 128))
    O["n_n"] = dout("n_n", (4, DEPTH, 2, 4, 128))
    O["n_m"] = dout("n_m", (4, DEPTH, 2, 4))
    if debug:
        for nm, shp in debug.items():
            O[nm] = dout(nm, shp)

    st = ExitStack()
    with st:
        def sbt(name, shape, dt):
            return Buf(st.enter_context(nc.sbuf_tensor(name, list(shape), dt)).ap())

        xT = sbt("xT", (128, 8, 2048), F32)
        yT = sbt("yT", (128, 8, 2048), BF16)
        ident = sbt("ident", (128, 128), F32)
        identb = sbt("identb", (128, 128), BF16)
        onesb = sbt("onesb", (128, 128), BF16)
        bd64 = sbt("bd64", (128, 128), BF16)
        sel = sbt("sel", (4, 2, 128), F32)
        nsel = sbt("nsel", (4, 1, 128), F32)
        oh = sbt("oh", (4, 4), F32)
        modT = sbt("modT", (128, DEPTH, 6, 8, 2), F32)
        gn = sbt("gn", (128, DEPTH, 2, 8), F32)
        gfin = sbt("gfin", (128, 8), F32)
        AB = sbt("AB", (128, 4, 8), F32)
        m0t = sbt("m0t", (4, DEPTH * 2), F32)
        AW = 27500
        arena = st.enter_context(nc.sbuf_tensor("arena", [128, AW], F32)).ap()
        gsc = Buf(nc.dram_tensor("gsc", [2, 4, 4, 2048], F32, kind="Internal").ap())
        psb = [Buf(st.enter_context(nc.psum_tensor(f"ps{i}", [128, 512], F32)).ap()) for i in range(8)]
        apos = [0]
        peak = [0]
        pctr = [0]

        def areset():
            P.barrier()
            peak[0] = max(peak[0], apos[0])
            if DEBUG_ARENA:
                print('arena used', apos[0])
            apos[0] = 0

        def alloc(shape, dt):
            n = int(np.prod(shape[1:]))
            words = n if dt == F32 else (n + 1) // 2
            words = (words + 7) // 8 * 8
            off = apos[0]
            apos[0] += words
            assert apos[0] <= AW, f"arena overflow {apos[0]} > {AW}"
            ap = arena[:, off:off + words]
            if dt == BF16:
                ap = ap.bitcast(BF16)
            ap = ap[:, 0:n]
            if len(shape) == 3:
                ap = ap.rearrange("p (a b) -> p a b", a=shape[1])
            elif len(shape) == 4:
                ap = ap.rearrange("p (a b c) -> p a b c", a=shape[1], b=shape[2])
            if shape[0] < 128:
                ap = ap[0:shape[0]]
            return Buf(ap)

        wscr = {}

        def wload_multi(items, is_s):
            if not (do_p and do_s):
                for key, tile, src in items:
                    P.dma("pool", tile, src, writes=[tile])
                return
            for key, tile, src in items:
                if key not in wscr:
                    nm = "ws_" + "_".join(str(k_) for k_ in key)
                    wscr[key] = Buf(nc.dram_tensor(nm, list(tile.ap.shape), BF16, kind="Internal").ap())
            if not is_s:
                for key, tile, src in items:
                    P.dma("pool", tile, src, writes=[tile])
                for key, tile, src in items:
                    P.dma("sp", wscr[key], tile, reads=[tile], writes=[wscr[key]])
            else:
                for key, tile, src in items:
                    P.dma("sp", tile, wscr[key], reads=[wscr[key]], writes=[tile])

        def wload(key, tile, src, is_s):
            wload_multi([(key, tile, src)], is_s)

        def psum():
            b = psb[pctr[0] % 8]
            pctr[0] += 1
            return b

        def pbf(b):
            return b.v(b.ap.bitcast(BF16))

        P.dma("sp", ident, I["k_ident"], writes=[ident])
        P.dma("pool", identb, I["k_ident"], writes=[identb])
        P.dma("sp", sel, I["k_sel"][:, 0:2, :], writes=[sel])
        P.dma("sp", nsel, I["k_nsel"][:, 3:4, :], writes=[nsel])
        P.dma("sp", oh, I["k_oh"], writes=[oh])
        P.memset("pool", onesb, 1.0)
        P.memset("pool", bd64, 0.0)
        P.memset("pool", bd64[0:64, 0:64], 1.0)
        P.memset("pool", bd64[64:128, 64:128], 1.0)
        for l in range(DEPTH):
            P.dma("sp", gn[:, l, 0, :], I["g_norm1"][l].rearrange("(k p) -> p k", p=128), writes=[gn])
            P.dma("sp", gn[:, l, 1, :], I["g_norm2"][l].rearrange("(k p) -> p k", p=128), writes=[gn])
        P.dma("sp", gfin, I["g_final"].rearrange("(k p) -> p k", p=128), writes=[gfin])
        P.dma("sp", m0t, I["m0"].rearrange("l d h -> h (l d)"), writes=[m0t])

        mod_done = [False]

        def do_modulation():
            mod_done[0] = True
            cT = alloc((128, 8, 2), F32)
            scT = alloc((128, 8, 2), BF16)
            bT = alloc((128, DEPTH, 48), F32)
            for w in range(2):
                P.dma("sp", cT[:, :, w], I["cvec"][w].rearrange("(k p) -> p k", p=128), writes=[cT])
            for l in range(DEPTH):
                P.dma("sp", bT[:, l, :], I["b_ada"][l].rearrange("(j p) -> p j", p=128), writes=[bT])
            P.act(scT, cT, AF.Silu)
            wq = [alloc((128, 8, 1536), BF16) for _ in range(2)]
            qi_ = 0
            for l in range(DEPTH):
                pm = psum()
                pmv = pm.v(pm.ap[:, 0:96].rearrange("p (j w) -> p j w", w=2))
                for qd in range(4):
                    wb = wq[qi_ % 2]
                    qi_ += 1
                    P.dma("pool", wb, I["w_ada"][l][:, qd * 1536:(qd + 1) * 1536].rearrange("(k p) c -> p k c", p=128), writes=[wb])
                    for jc in range(12):
                        j = qd * 12 + jc
                        for kc in range(8):
                            P.mm(pmv[:, j, :], wb[:, kc, jc * 128:(jc + 1) * 128], scT[:, kc, :], start=(kc == 0), stop=(kc == 7))
                for w in range(2):
                    P.tt("dve", modT.v(modT.ap[:, l, :, :, w].rearrange("p i k -> p (i k)")), pmv[:, :, w], bT[:, l, :], ALU.add)


        def run_group(gname, which, xin, yout, nseq, L, rope, ctx):
            T = nseq * L
            NB = T // 512
            NCH = T // 128
            nch_seq = L // 128
            Lk = L + (256 if ctx else 0)
            nkt = Lk // 128
            qblk = 256
            nqt = qblk // 128

            def X(kc, c0, c1):
                return xT[:, kc, c0:c1]

            def Y(kc, c0, c1):
                return yT[:, kc, c0:c1]

            areset()
            stg = [alloc((128, D), F32) for _ in range(2)]
            for tt_ in range(NCH):
                sg = stg[tt_ % 2]
                P.dma("sp", sg, xin[tt_ * 128:(tt_ + 1) * 128, :], writes=[sg])
                for half in range(2):
                    pp = psum()
                    for k4 in range(4):
                        kc = half * 4 + k4
                        P.tr(pp[:, k4 * 128:(k4 + 1) * 128], sg[:, kc * 128:(kc + 1) * 128], ident)
                    P.copy("act" if half == 0 else "dve",
                           xT.v(xT.ap[:, half * 4:half * 4 + 4, tt_ * 128:(tt_ + 1) * 128]),
                           pp.v(pp.ap.rearrange("p (a b) -> p a b", a=4)))
            if not mod_done[0]:
                do_modulation()

            def norm_mod(Acol, Bcol):
                sq = [alloc((128, 8, 512), BF16) for _ in range(2)]
                rs = [alloc((128, 512), F32) for _ in range(2)]
                tmp = [alloc((128, 512), F32) for _ in range(3)]
                ti = 0
                for tb in range(NB):
                    c0, c1 = tb * 512, (tb + 1) * 512
                    s_ = sq[tb % 2]
                    r_ = rs[tb % 2]
                    P.tt("pool", s_, xT[:, :, c0:c1], xT[:, :, c0:c1], ALU.mult)
                    pss = psum()
                    for kc in range(8):
                        P.mm(pss, onesb, s_[:, kc, :], start=(kc == 0), stop=(kc == 7))
                    P.act(r_, pss, AF.Ln, bias=float(D * EPS))
                    P.act(r_, r_, AF.Exp, scale=-0.5)
                    for kc in range(8):
                        t_ = tmp[ti % 3]
                        ti += 1
                        P.stt("dve", t_, X(kc, c0, c1), Acol(kc), r_, ALU.mult, ALU.mult)
                        if Bcol is None:
                            P.copy("act", Y(kc, c0, c1), t_)
                        else:
                            P.act(Y(kc, c0, c1), t_, AF.Identity, bias=Bcol(kc))

            def mcol(l, i, kc):
                return modT[:, l, i, kc, which:which + 1]

            def outproj(l, oT, row0, nkc):
                wo = alloc((128, nkc, D), BF16)
                wload(("wo", l, row0), wo, I["w_out"][l][row0:row0 + nkc * 128, :].rearrange("(k p) c -> p k c", p=128), ctx)
                for tb in range(NB):
                    c0, c1 = tb * 512, (tb + 1) * 512
                    for m in range(8):
                        pp = psum()
                        for kc in range(nkc):
                            P.mm(pp, wo[:, kc, m * 128:(m + 1) * 128], oT[:, kc, c0:c1], start=(kc == 0), stop=(kc == nkc - 1))
                        P.stt("dve", X(m, c0, c1), pp, mcol(l, 2, m), X(m, c0, c1), ALU.mult, ALU.add)

            for l in range(DEPTH):
                areset()
                for i_, (gi, si) in enumerate(((0, 1), (1, 4))):
                    P.ts("dve", AB[:, i_, :], modT.v(modT.ap[:, l, si, :, which]), 1.0, ALU.add, 32.0, ALU.mult)
                    P.tt("dve", AB[:, i_, :], AB[:, i_, :], gn[:, l, gi, :], ALU.mult)
                norm_mod(lambda kc: AB[:, 0, kc:kc + 1], lambda kc: mcol(l, 0, kc))

                if not ctx:
                    areset()
                    wtm = alloc((128, 8, 416), BF16)
                    P.dma("pool", wtm, I["wx"][l][:, 3040:3456].rearrange("(k p) c -> p k c", p=128), writes=[wtm])
                    gkv_bc = alloc((128, 128), F32)
                    gk_bc = alloc((128, 64), F32)
                    P.dma("sp", gkv_bc, I["g_mla_kv"][l].partition_broadcast(128), writes=[gkv_bc])
                    P.dma("sp", gk_bc, I["g_gqa_k"][l].partition_broadcast(128), writes=[gk_bc])
                    stg2 = [alloc((128, 416), F32) for _ in range(2)]
                    junk = alloc((128, 128), F32)
                    st3 = [alloc((128, 4), F32) for _ in range(2)]
                    for tt_ in range(NCH):
                        sq_, so = stg2[tt_ % 2], st3[tt_ % 2]
                        s_i, tk = divmod(tt_, nch_seq)
                        pp = psum()
                        for kc in range(8):
                            P.mm(pp[:, 0:416], Y(kc, tt_ * 128, (tt_ + 1) * 128), wtm[:, kc, :], start=(kc == 0), stop=(kc == 7))
                        for gi_, (a, b) in enumerate(((0, 128), (160, 224), (224, 288))):
                            P.act(junk[:, 0:b - a], pp[:, a:b], AF.Square, accum=so[:, gi_:gi_ + 1])
                        P.act(so[:, 0:1], so[:, 0:1], AF.Ln, bias=EPS, scale=1.0 / 128)
                        P.act(so[:, 1:3], so[:, 1:3], AF.Ln, bias=EPS, scale=1.0 / 64)
                        P.act(so[:, 0:3], so[:, 0:3], AF.Exp, scale=-0.5)
                        P.stt("dve", sq_[:, 0:128], pp[:, 0:128], so[:, 0:1], gkv_bc, ALU.mult, ALU.mult)
                        P.copy("act", sq_[:, 128:160], pp[:, 128:160])
                        P.stt("dve", sq_[:, 160:224], pp[:, 160:224], so[:, 1:2], gk_bc, ALU.mult, ALU.mult)
                        P.stt("dve", sq_[:, 224:288], pp[:, 224:288], so[:, 2:3], gk_bc, ALU.mult, ALU.mult)
                        P.copy("act", sq_[:, 288:416], pp[:, 288:416])
                        r0, r1 = tk * 128, (tk + 1) * 128
                        P.dma("sp", O["n_ckv"][s_i, l, r0:r1, :], sq_[:, 0:128], reads=[sq_])
                        P.dma("sp", O["n_kr"][s_i, l, r0:r1, :], sq_[:, 128:160], reads=[sq_])
                        P.dma("sp", O["n_k"][s_i, l, r0:r1, :], sq_[:, 160:288], reads=[sq_])
                        P.dma("sp", O["n_v"][s_i, l, r0:r1, :], sq_[:, 288:416], reads=[sq_])

                areset()
                wA = alloc((128, 8, 576), BF16)
                wload(("wA", l), wA, I["wx"][l][:, 0:576].rearrange("(k p) c -> p k c", p=128), ctx)
                wuq = alloc((128, 2, 768), BF16)
                wload(("wuq", l), wuq, I["uqx"][l].rearrange("(k p) c -> p k c", p=128), ctx)
                wukv = alloc((128, 512), BF16)
                wload(("wukv", l), wukv, I["ukvx"][l], ctx)
                gqc = alloc((128, 2), F32)
                P.dma("sp", gqc, I["g_mla_q"][l].rearrange("(k p) -> p k", p=128), writes=[gqc])
                gkvc = alloc((128, 1), F32)
                P.dma("sp", gkvc, I["g_mla_kv"][l].rearrange("(p o) -> p o", o=1), writes=[gkvc])
                rtabs = [alloc((128, 2, 512), F32) for _ in range(2)]
                rti = [0]

                def load_rt(src, t0, n):
                    r_ = rtabs[rti[0] % 2]
                    rti[0] += 1
                    P.dma("sp", r_[:, :, 0:n], I[src][:, :, t0:t0 + n], writes=[r_])
                    return r_
                ckvT = alloc((128, nseq, Lk), BF16)
                khT = [alloc((96, nseq, Lk), BF16) for _ in range(4)]
                krT = khT[0]
                vaug = alloc((128, nseq * nkt, 4, 65), BF16)
                P.memset("pool", vaug[:, :, :, 64:65], 1.0)
                sqb = [alloc((128, 512), BF16) for _ in range(2)]
                rsb = [alloc((128, 512), F32) for _ in range(2)]
                tA = [alloc((128, 512), F32)] * 2
                tB = [alloc((128, 512), F32)] * 2
                for tb in range(NB):
                    c0, c1 = tb * 512, (tb + 1) * 512
                    nsq = 512 // L if L < 512 else 1
                    s0 = c0 // L
                    o0 = c0 % L

                    def dst(tile):
                        if L >= 512:
                            return tile.v(tile.ap[:, s0, o0:o0 + 512])
                        return tile.v(tile.ap[:, s0:s0 + nsq, 0:L])

                    def as3(b_):
                        if L >= 512:
                            return b_
                        return b_.v(b_.ap.rearrange("p (a b) -> p a b", a=nsq))
                    pr = psum()
                    for kc in range(8):
                        P.mm(pr, wA[:, kc, 256:384], Y(kc, c0, c1), start=(kc == 0), stop=(kc == 7))
                    sq_, r_ = sqb[tb % 2], rsb[tb % 2]
                    P.act(sq_, pr, AF.Square)
                    pss = psum()
                    P.mm(pss, onesb, sq_)
                    P.act(r_, pss, AF.Ln, bias=EPS, scale=1.0 / 128)
                    P.act(r_, r_, AF.Exp, scale=-0.5)
                    P.stt("dve", dst(ckvT), as3(pr), gkvc[:, 0:1], as3(r_), ALU.mult, ALU.mult)
                    p1 = psum()
                    for kc in range(8):
                        P.mm(p1[0:96, :], wA[:, kc, 384:480], Y(kc, c0, c1), start=(kc == 0), stop=(kc == 7))
                    if rope:
                        p2 = psum()
                        for kc in range(8):
                            P.mm(p2[0:96, :], wA[:, kc, 480:576], Y(kc, c0, c1), start=(kc == 0), stop=(kc == 7))
                        a_, b_ = tA[tb % 2], tB[tb % 2]
                        rtab = load_rt("k_ropem", c0, 512)
                        P.tt("dve", a_[64:96, :], p1[64:96, :], rtab[64:96, 0, :], ALU.mult)
                        P.tt("dve", b_[64:96, :], p2[64:96, :], rtab[64:96, 1, :], ALU.mult)
                        P.tt("pool", dst(krT)[64:96], a_[64:96, :], b_[64:96, :], ALU.add)
                    else:
                        P.copy("act", dst(krT)[64:96], as3(p1)[64:96])
                if ctx:
                    cst = alloc((128, 2, 128), F32)
                    P.dma("sp", cst, I["c_ckv"][l].rearrange("(t p) c -> p t c", p=128), writes=[cst])
                    cs2 = alloc((128, 2, 96), F32)
                    P.memset("pool", cs2, 0.0)
                    P.dma("sp", cs2[:, :, 64:96], I["c_kr"][l].rearrange("(t p) c -> p t c", p=128), writes=[cs2])
                    for t2 in range(2):
                        pp = psum()
                        P.tr(pp[:, 0:128], cst[:, t2, :], ident)
                        P.copy("act", ckvT[:, 0, L + t2 * 128:L + (t2 + 1) * 128], pp[:, 0:128])
                        pp = psum()
                        P.tr(pp[0:96, 0:128], cs2[:, t2, :], ident)
                        P.copy("act", krT[64:96, 0, L + t2 * 128:L + (t2 + 1) * 128], pp[64:96, 0:128])
                for s in range(nseq):
                    for kt in range(nkt):
                        pp = psum()
                        P.mm(pp[:, 0:256], ckvT[:, s, kt * 128:(kt + 1) * 128], wukv[:, 256:512])
                        P.copy("act" if kt % 2 else "dve", vaug[:, s * nkt + kt, :, 0:64],
                               pp.v(pp.ap[:, 0:256].rearrange("p (h d) -> p h d", h=4)))
                for h in range(4):
                    if h > 0:
                        P.copy("pool", khT[h][64:96], krT[64:96])
                    for s in range(nseq):
                        for k0 in range(0, Lk, 512):
                            k1 = min(Lk, k0 + 512)
                            pp = psum()
                            P.mm(pp[0:64, 0:k1 - k0], wukv[:, h * 64:(h + 1) * 64], ckvT[:, s, k0:k1])
                            P.copy("act" if h % 2 else "dve", khT[h][0:64, s, k0:k1], pp[0:64, 0:k1 - k0])
                oT = alloc((128, 2, T), BF16)
                cqn = [alloc((128, 2, qblk), BF16) for _ in range(2)]
                qh = [alloc((96, qblk), BF16) for _ in range(2)]
                PT = [alloc((128, nkt, qblk), BF16) for _ in range(2)]
                otok = [alloc((128, nqt, 256), BF16) for _ in range(2)]
                rcp = [alloc((128, 4), F32) for _ in range(2)]
                sq2 = [alloc((128, 2, qblk), BF16) for _ in range(2)]
                units = [(s, q0, h) for s in range(nseq) for q0 in range(0, L, qblk) for h in range(4)]
                blk = {}
                uctx = {}
                cnt = {"qb": 0, "h": 0}

                blocks = [(s, q0) for s in range(nseq) for q0 in range(0, L, qblk)]

                def mla_pro(s, q0):
                    g0 = s * L + q0
                    if True:
                        qbi = cnt["qb"]
                        cnt["qb"] += 1
                        cq_, ot_ = cqn[qbi % 2], otok[qbi % 2]
                        s2_, r_ = sq2[qbi % 2], rsb[qbi % 2]
                        rtq = load_rt("k_ropem", q0, qblk) if rope else None
                        pc = [psum(), psum()]
                        for c in range(2):
                            for kc in range(8):
                                P.mm(pc[c][:, 0:qblk], wA[:, kc, c * 128:(c + 1) * 128], Y(kc, g0, g0 + qblk), start=(kc == 0), stop=(kc == 7))
                            P.act(s2_[:, c, :], pc[c][:, 0:qblk], AF.Square)
                        pss = psum()
                        for c in range(2):
                            P.mm(pss[:, 0:qblk], onesb, s2_[:, c, :], start=(c == 0), stop=(c == 1))
                        P.act(r_[:, 0:qblk], pss[:, 0:qblk], AF.Ln, bias=EPS, scale=1.0 / 256)
                        P.act(r_[:, 0:qblk], r_[:, 0:qblk], AF.Exp, scale=-0.5)
                        for c in range(2):
                            P.stt("dve", cq_[:, c, :], pc[c][:, 0:qblk], gqc[:, c:c + 1], r_[:, 0:qblk], ALU.mult, ALU.mult)
                        blk[(s, q0)] = (cq_, ot_, rtq)

                def mla_A(u):
                    s, q0, h = u
                    g0 = s * L + q0
                    if (s, q0) not in blk:
                        mla_pro(s, q0)
                    if h == 2:
                        bi_ = blocks.index((s, q0))
                        if bi_ + 1 < len(blocks) and blocks[bi_ + 1] not in blk:
                            mla_pro(*blocks[bi_ + 1])
                    cq_, ot_, rtq = blk[(s, q0)]
                    hi_ = cnt["h"]
                    cnt["h"] += 1
                    q_, pt_, rc_ = qh[hi_ % 2], PT[hi_ % 2], rcp[hi_ % 2]
                    p1 = psum()
                    for c in range(2):
                        P.mm(p1[0:96, 0:qblk], wuq[:, c, h * 192:h * 192 + 96], cq_[:, c, :], start=(c == 0), stop=(c == 1))
                    P.copy("dve", q_[0:64, :], p1[0:64, 0:qblk])
                    if rope:
                        p2 = psum()
                        for c in range(2):
                            P.mm(p2[0:96, 0:qblk], wuq[:, c, h * 192 + 96:h * 192 + 192], cq_[:, c, :], start=(c == 0), stop=(c == 1))
                        a_, b_ = tA[hi_ % 2], tB[hi_ % 2]
                        P.tt("dve", a_[64:96, 0:qblk], p1[64:96, 0:qblk], rtq[64:96, 0, 0:qblk], ALU.mult)
                        P.tt("dve", b_[64:96, 0:qblk], p2[64:96, 0:qblk], rtq[64:96, 1, 0:qblk], ALU.mult)
                        P.tt("pool", q_[64:96, :], a_[64:96, 0:qblk], b_[64:96, 0:qblk], ALU.add)
                    else:
                        P.copy("dve", q_[64:96, :], p1[64:96, 0:qblk])
                    uctx0[u] = (q_, pt_, rc_, ot_)

                def mla_A1(u):
                    s, q0, h = u
                    q_, pt_, rc_, ot_ = uctx0.pop(u)
                    for kt in range(0, nkt, 2):
                        pp = psum()
                        for k2 in range(2):
                            P.mm(pp[:, k2 * qblk:(k2 + 1) * qblk], khT[h][0:96, s, (kt + k2) * 128:(kt + k2 + 1) * 128], q_[0:96, :])
                        P.act(pt_[:, kt:kt + 2, :], pp.v(pp.ap[:, 0:2 * qblk].rearrange("p (a b) -> p a b", a=2)), AF.Exp, scale=float(96 ** -0.5))
                    uctx[u] = (pt_, rc_, ot_)

                def mla_B(u):
                    s, q0, h = u
                    g0 = s * L + q0
                    pt_, rc_, ot_ = uctx.pop(u)
                    pacc = psum()
                    for j in range(nqt):
                        for kt in range(nkt):
                            P.mm(pacc[:, j * 65:(j + 1) * 65], pt_[:, kt, j * 128:(j + 1) * 128], vaug[:, s * nkt + kt, h, :],
                                 start=(kt == 0), stop=(kt == nkt - 1))
                    pav = pacc.v(pacc.ap[:, 0:nqt * 65].rearrange("p (j d) -> p j d", d=65))
                    P.recip(rc_[:, 0:nqt], pav[:, :, 64])
                    for j in range(nqt):
                        P.ts("dve", ot_[:, j, h * 64:(h + 1) * 64], pav[:, j, 0:64], rc_[:, j:j + 1], ALU.mult)
                    if h == 3:
                        for j in range(nqt):
                            pp = psum()
                            ppb = pbf(pp)
                            for c in range(2):
                                P.tr(ppb[:, c * 128:(c + 1) * 128], ot_[:, j, c * 128:(c + 1) * 128], identb)
                            P.copy("dve", oT[:, :, g0 + j * 128:g0 + (j + 1) * 128],
                                   ppb.v(ppb.ap[:, 0:256].rearrange("p (c t) -> p c t", c=2)))

                uctx0 = {}
                mla_A(units[0])
                if len(units) > 1:
                    mla_A(units[1])
                mla_A1(units[0])
                for ui, u in enumerate(units):
                    if ui + 2 < len(units):
                        mla_A(units[ui + 2])
                    if ui + 1 < len(units):
                        mla_A1(units[ui + 1])
                    mla_B(u)
                outproj(l, oT, 0, 2)
                if debug and "dbg_x" in debug and l == 0 and debug.get("_stage") == "mla":
                    pass

                areset()
                wC = alloc((128, 8, 896), BF16)
                wload(("wC", l), wC, I["wx"][l][:, 576:1472].rearrange("(k p) c -> p k c", p=128), ctx)
                gqt = alloc((128, 4), F32)
                P.dma("sp", gqt, I["gq"][l], writes=[gqt])
                rtabs = [alloc((128, 2, 512), F32) for _ in range(2)]
                rti = [0]

                def load_rt(src, t0, n):
                    r_ = rtabs[rti[0] % 2]
                    rti[0] += 1
                    P.dma("sp", r_[:, :, 0:n], I[src][:, :, t0:t0 + n], writes=[r_])
                    return r_
                kT = alloc((128, nseq, Lk), BF16)
                vag = alloc((128, nseq * nkt, 2, 65), BF16)
                P.memset("pool", vag[:, :, :, 64:65], 1.0)
                sqb = [alloc((128, 512), BF16) for _ in range(2)]
                rsb = [alloc((128, 512), F32) for _ in range(2)]
                tA = [alloc((128, 512), F32) for _ in range(2)]
                tB = [alloc((128, 512), F32) for _ in range(2)]
                nrm_i = [0]

                def qk_norm_rope(dst_, wcol, wcol_sw, gi, c0, n, tcol0):
                    i_ = nrm_i[0]
                    nrm_i[0] += 1
                    sq_, r_, a_, b_ = sqb[i_ % 2], rsb[i_ % 2], tA[i_ % 2], tB[i_ % 2]
                    p1 = psum()
                    for kc in range(8):
                        P.mm(p1[:, 0:n], wC[:, kc, wcol:wcol + 128], Y(kc, c0, c0 + n), start=(kc == 0), stop=(kc == 7))
                    P.act(sq_[:, 0:n], p1[:, 0:n], AF.Square)
                    pss = psum()
                    P.mm(pss[:, 0:n], bd64, sq_[:, 0:n])
                    P.act(r_[:, 0:n], pss[:, 0:n], AF.Ln, bias=EPS, scale=1.0 / 64)
                    P.act(r_[:, 0:n], r_[:, 0:n], AF.Exp, scale=-0.5)
                    if not rope:
                        P.stt("dve", dst_, p1[:, 0:n], gqt[:, gi:gi + 1], r_[:, 0:n], ALU.mult, ALU.mult)
                        return
                    p2 = psum()
                    for kc in range(8):
                        P.mm(p2[:, 0:n], wC[:, kc, wcol_sw:wcol_sw + 128], Y(kc, c0, c0 + n), start=(kc == 0), stop=(kc == 7))
                    P.stt("dve", a_[:, 0:n], p1[:, 0:n], gqt[:, gi:gi + 1], r_[:, 0:n], ALU.mult, ALU.mult)
                    P.stt("dve", b_[:, 0:n], p2[:, 0:n], gqt[:, gi + 1:gi + 2], r_[:, 0:n], ALU.mult, ALU.mult)
                    rtab = load_rt("k_ropeg", tcol0, n)
                    P.tt("pool", a_[:, 0:n], a_[:, 0:n], rtab[:, 0, 0:n], ALU.mult)
                    P.tt("pool", b_[:, 0:n], b_[:, 0:n], rtab[:, 1, 0:n], ALU.mult)
                    P.tt("dve", dst_, a_[:, 0:n], b_[:, 0:n], ALU.add)

                for s in range(nseq):
                    for k0 in range(0, L, 512):
                        n = min(512, L - k0)
                        qk_norm_rope(kT[:, s, k0:k0 + n], 512, 640, 2, s * L + k0, n, k0)
                    for kt in range(L // 128):
                        pp = psum()
                        for kc in range(8):
                            P.mm(pp[:, 0:128], Y(kc, s * L + kt * 128, s * L + (kt + 1) * 128), wC[:, kc, 768:896], start=(kc == 0), stop=(kc == 7))
                        P.copy("act", vag[:, s * nkt + kt, :, 0:64], pp.v(pp.ap[:, 0:128].rearrange("p (h d) -> p h d", h=2)))
                if ctx:
                    cst = alloc((128, 2, 128), F32)
                    P.dma("sp", cst, I["c_gk"][l].rearrange("(t p) c -> p t c", p=128), writes=[cst])
                    cv = alloc((128, 2, 128), F32)
                    P.dma("sp", cv, I["c_gv"][l].rearrange("(t p) c -> p t c", p=128), writes=[cv])
                    for t2 in range(2):
                        pp = psum()
                        P.tr(pp[:, 0:128], cst[:, t2, :], ident)
                        P.copy("act", kT[:, 0, L + t2 * 128:L + (t2 + 1) * 128], pp[:, 0:128])
                        P.copy("dve", vag[:, L // 128 + t2, :, 0:64], cv.v(cv.ap[:, t2, :].rearrange("p (h d) -> p h d", h=2)))
                oT = alloc((128, 2, T), BF16)
                qc = [[alloc((128, qblk), BF16) for _ in range(2)] for _ in range(2)]
                PT = [alloc((128, nkt, qblk), BF16) for _ in range(2)]
                otok = [alloc((128, nqt, 256), BF16) for _ in range(2)]
                rcp = [alloc((128, 4), F32) for _ in range(2)]
                units = [(s, q0, h) for s in range(nseq) for q0 in range(0, L, qblk) for h in range(4)]
                blk = {}
                uctx = {}
                cnt = {"qb": 0, "h": 0}

                blocks = [(s, q0) for s in range(nseq) for q0 in range(0, L, qblk)]

                def gqa_pro(s, q0):
                    g0 = s * L + q0
                    qbi = cnt["qb"]
                    cnt["qb"] += 1
                    qq, ot_ = qc[qbi % 2], otok[qbi % 2]
                    qk_norm_rope(qq[0], 0, 256, 0, g0, qblk, q0)
                    qk_norm_rope(qq[1], 128, 384, 0, g0, qblk, q0)
                    blk[(s, q0)] = (qq, ot_)

                def gqa_A(u):
                    s, q0, h = u
                    g0 = s * L + q0
                    if (s, q0) not in blk:
                        gqa_pro(s, q0)
                    if h == 2:
                        bi_ = blocks.index((s, q0))
                        if bi_ + 1 < len(blocks) and blocks[bi_ + 1] not in blk:
                            gqa_pro(*blocks[bi_ + 1])
                    qq, ot_ = blk[(s, q0)]
                    hi_ = cnt["h"]
                    cnt["h"] += 1
                    pt_, rc_ = PT[hi_ % 2], rcp[hi_ % 2]
                    qsrc = qq[h % 2]
                    r0 = 0 if h < 2 else 64
                    for kt in range(0, nkt, 2):
                        pp = psum()
                        for k2 in range(2):
                            P.mm(pp[:, k2 * qblk:(k2 + 1) * qblk], kT[r0:r0 + 64, s, (kt + k2) * 128:(kt + k2 + 1) * 128], qsrc[r0:r0 + 64, :])
                        P.act(pt_[:, kt:kt + 2, :], pp.v(pp.ap[:, 0:2 * qblk].rearrange("p (a b) -> p a b", a=2)), AF.Exp, scale=float(64 ** -0.5))
                    uctx[u] = (pt_, rc_, ot_)

                def gqa_B(u):
                    s, q0, h = u
                    g0 = s * L + q0
                    kvh = h // 2
                    pt_, rc_, ot_ = uctx.pop(u)
                    pacc = psum()
                    for j in range(nqt):
                        for kt in range(nkt):
                            P.mm(pacc[:, j * 65:(j + 1) * 65], pt_[:, kt, j * 128:(j + 1) * 128], vag[:, s * nkt + kt, kvh, :],
                                 start=(kt == 0), stop=(kt == nkt - 1))
                    pav = pacc.v(pacc.ap[:, 0:nqt * 65].rearrange("p (j d) -> p j d", d=65))
                    P.recip(rc_[:, 0:nqt], pav[:, :, 64])
                    for j in range(nqt):
                        P.ts("dve", ot_[:, j, h * 64:(h + 1) * 64], pav[:, j, 0:64], rc_[:, j:j + 1], ALU.mult)
                    if h == 3:
                        for j in range(nqt):
                            pp = psum()
                            ppb = pbf(pp)
                            for c in range(2):
                                P.tr(ppb[:, c * 128:(c + 1) * 128], ot_[:, j, c * 128:(c + 1) * 128], identb)
                            P.copy("dve", oT[:, :, g0 + j * 128:g0 + (j + 1) * 128],
                                   ppb.v(ppb.ap[:, 0:256].rearrange("p (c t) -> p c t", c=2)))

                gqa_A(units[0])
                for ui, u in enumerate(units):
                    if ui + 1 < len(units):
                        gqa_A(units[ui + 1])
                    gqa_B(u)
                outproj(l, oT, 768, 2)

                areset()
                mlstm(l)

                areset()
                norm_mod(lambda kc: AB[:, 1, kc:kc + 1], lambda kc: mcol(l, 3, kc))
                areset()
                ffn(l)

            areset()
            P.ts("dve", AB[:, 2, :], gfin, 32.0, ALU.mult)
            sq = [alloc((128, 8, 512), BF16) for _ in range(2)]
            rs = [alloc((128, 512), F32) for _ in range(2)]
            xn = [alloc((128, 8, 512), F32) for _ in range(2)]
            ost = [alloc((128, D), F32) for _ in range(2)]
            oi = 0
            for tb in range(NB):
                c0, c1 = tb * 512, (tb + 1) * 512
                s_, r_, xn_ = sq[tb % 2], rs[tb % 2], xn[tb % 2]
                P.tt("pool", s_, xT[:, :, c0:c1], xT[:, :, c0:c1], ALU.mult)
                pss = psum()
                for kc in range(8):
                    P.mm(pss, onesb, s_[:, kc, :], start=(kc == 0), stop=(kc == 7))
                P.act(r_, pss, AF.Ln, bias=float(D * EPS))
                P.act(r_, r_, AF.Exp, scale=-0.5)
                for kc in range(8):
                    P.stt("dve", xn_[:, kc, :], X(kc, c0, c1), AB[:, 2, kc:kc + 1], r_, ALU.mult, ALU.mult)
                for j in range(4):
                    o_ = ost[oi % 2]
                    oi += 1
                    for half in range(2):
                        pp = psum()
                        for k4 in range(4):
                            kc = half * 4 + k4
                            P.tr(pp[:, k4 * 128:(k4 + 1) * 128], xn_[:, kc, j * 128:(j + 1) * 128], ident)
                        P.copy("act" if half == 0 else "dve", o_[:, half * 512:(half + 1) * 512], pp)
                    P.dma("sp", yout[c0 + j * 128:c0 + (j + 1) * 128, :], o_, reads=[o_])

            return

        mlstm = None
        ffn = None
        G = {}

        def make_stage_fns(which, nseq, L, rope, ctx):
            T = nseq * L
            NB = T // 512
            NCH = T // 128
            nch_seq = L // 128

            def X(kc, c0, c1):
                return xT[:, kc, c0:c1]

            def Y(kc, c0, c1):
                return yT[:, kc, c0:c1]

            def mcol(l, i, kc):
                return modT[:, l, i, kc, which:which + 1]

            def ffn_(l):
                SEGT = 1024
                segs = []
                if L >= SEGT:
                    for s in range(nseq):
                        for a_ in range(0, L, SEGT):
                            segs.append((s * L + a_, 1, SEGT, (s * L + a_ - 1) if a_ > 0 else None,
                                         (s * L + a_ + SEGT) if a_ + SEGT < L else None))
                else:
                    per = SEGT // L
                    for s0 in range(0, nseq, per):
                        segs.append((s0 * L, per, L, None, None))
                halves = [(0, 12), (12, 22)]
                nsub, Ls = segs[0][1], segs[0][2]
                wup = [alloc((128, 8, 2, 256), BF16) for _ in range(2)]
                wdn = [alloc((128, 12, D), BF16), alloc((128, 10, D), BF16)]
                cwt = alloc((128, 22, 3), F32)
                cbt = alloc((128, 22), F32)
                for j_ in range(3):
                    P.dma("sp", cwt[:, :, j_], I["w_ff_conv"][l, j_].rearrange("(c p) -> p c", p=128), writes=[cwt])
                P.dma("sp", cbt, I["b_ff_conv"][l].rearrange("(c p) -> p c", p=128), writes=[cbt])
                gs = [alloc((128, nsub, Ls + 2), F32) for _ in range(2)]
                for g_ in gs:
                    P.memset("pool", g_[:, :, 0:1], 0.0)
                    P.memset("pool", g_[:, :, Ls + 1:Ls + 2], 0.0)
                asb = [alloc((128, SEGT), BF16) for _ in range(2)]
                tcv = [alloc((128, nsub, Ls), F32) for _ in range(2)]
                hT = alloc((128, 12, SEGT), BF16)
                ci = 0
                tasks = [(si_, hi, pr) for si_ in range(len(segs)) for hi, (j0, j1) in enumerate(halves) for pr in range(j0 // 2, j1 // 2)]

                def load_wu(k):
                    if k >= len(tasks):
                        return
                    pr_ = tasks[k][2]
                    wu_ = wup[k % 2]
                    wload_multi([(("wu", l, pr_, half), wu_[:, :, half, :], I["w_ff_up"][l][:, half * DFF + pr_ * 256:half * DFF + pr_ * 256 + 256].rearrange("(k p) c -> p k c", p=128)) for half in range(2)], ctx)
                load_wu(0)
                tk = 0
                pending = []
                for (c0, _ns, _ls, lh, rh) in segs:
                    for hi, (j0, j1) in enumerate(halves):
                        wd = wdn[hi]
                        nj = j1 - j0
                        wload(("wd", l, hi), wd[:, 0:nj, :], I["w_ff_down"][l][j0 * 128:j1 * 128, :].rearrange("(k p) c -> p k c", p=128), ctx)
                        for pr in range(j0 // 2, j1 // 2):
                            wu = wup[tk % 2]
                            load_wu(tk + 1)
                            tk += 1
                            bufs = []
                            for cc in range(2):
                                ch = pr * 2 + cc
                                j = ch - j0
                                g_, a_, t_ = gs[ci % 2], asb[ci % 2], tcv[ci % 2]
                                ci += 1
                                bufs.append((ch, j, g_, a_, t_))
                                for tb in range(SEGT // 512):
                                    t0 = c0 + tb * 512
                                    pa, pg = psum(), psum()
                                    for kc in range(8):
                                        P.mm(pa, wu[:, kc, 0, cc * 128:(cc + 1) * 128], Y(kc, t0, t0 + 512), start=(kc == 0), stop=(kc == 7))
                                    for kc in range(8):
                                        P.mm(pg, wu[:, kc, 1, cc * 128:(cc + 1) * 128], Y(kc, t0, t0 + 512), start=(kc == 0), stop=(kc == 7))
                                    P.copy("act", a_[:, tb * 512:(tb + 1) * 512], pa)
                                    if Ls >= 512:
                                        P.copy("act", g_[:, 0, 1 + tb * 512:1 + (tb + 1) * 512], pg)
                                    else:
                                        nsq = 512 // Ls
                                        P.copy("act", g_[:, tb * nsq:(tb + 1) * nsq, 1:Ls + 1], pg.v(pg.ap.rearrange("p (a b) -> p a b", a=nsq)))
                                if L >= SEGT:
                                    for hc, col in ((lh, 0), (rh, Ls + 1)):
                                        if hc is None:
                                            P.memset("pool", g_[:, 0, col:col + 1], 0.0)
                                        else:
                                            ph = psum()
                                            for kc in range(8):
                                                P.mm(ph[:, 0:1], wu[:, kc, 1, cc * 128:(cc + 1) * 128], Y(kc, hc, hc + 1), start=(kc == 0), stop=(kc == 7))
                                            P.copy("act", g_[:, 0, col:col + 1], ph[:, 0:1])
                            if pending:
                                pending.pop()()
                            for (ch, j, g_, a_, t_) in bufs:
                                P.ts("dve", t_, g_[:, :, 1:Ls + 1], cwt[:, ch, 1:2], ALU.mult, cbt[:, ch:ch + 1], ALU.add)
                                P.stt("dve", t_, g_[:, :, 0:Ls], cwt[:, ch, 0:1], t_, ALU.mult, ALU.add)
                                P.stt("dve", t_, g_[:, :, 2:Ls + 2], cwt[:, ch, 2:3], t_, ALU.mult, ALU.add)
                                P.act(hT[:, j, :], t_.v(t_.ap.rearrange("p a b -> p (a b)")), AF.Silu)
                                P.tt("pool", hT[:, j, :], hT[:, j, :], a_, ALU.mult)

                        def down(c0=c0, wd=wd, nj=nj):
                            for tb in range(SEGT // 512):
                                t0 = c0 + tb * 512
                                for m in range(8):
                                    pp = psum()
                                    for j in range(nj):
                                        P.mm(pp, wd[:, j, m * 128:(m + 1) * 128], hT[:, j, tb * 512:(tb + 1) * 512], start=(j == 0), stop=(j == nj - 1))
                                    P.stt("dve", X(m, t0, t0 + 512), pp, mcol(l, 5, m), X(m, t0, t0 + 512), ALU.mult, ALU.add)
                        pending.append(down)
                if pending:
                    pending.pop()()

            def outproj(l, oT, row0, nkc):
                wo = alloc((128, nkc, D), BF16)
                wload(("wo", l, row0), wo, I["w_out"][l][row0:row0 + nkc * 128, :].rearrange("(k p) c -> p k c", p=128), ctx)
                for tb in range(NB):
                    c0, c1 = tb * 512, (tb + 1) * 512
                    for m in range(8):
                        pp = psum()
                        for kc in range(nkc):
                            P.mm(pp, wo[:, kc, m * 128:(m + 1) * 128], oT[:, kc, c0:c1], start=(kc == 0), stop=(kc == nkc - 1))
                        P.stt("dve", X(m, c0, c1), pp, mcol(l, 2, m), X(m, c0, c1), ALU.mult, ALU.add)

            def mlstm_(l):
                wg = alloc((128, 8, 16), BF16)
                wload(("wg", l), wg, I["wx"][l][:, 3008:3024].rearrange("(k p) c -> p k c", p=128), ctx)
                bgt = alloc((4, 4), F32)
                P.dma("sp", bgt, I["bg"][l], writes=[bgt])
                Gt = [alloc((4, T), F32) for _ in range(4)]
                for ty in range(4):
                    for tb in range(NB):
                        t0, t1 = tb * 512, (tb + 1) * 512
                        pp = psum()
                        for kc in range(8):
                            P.mm(pp[0:4, :], wg[:, kc, ty * 4:(ty + 1) * 4], Y(kc, t0, t1), start=(kc == 0), stop=(kc == 7))
                        P.act(Gt[ty][:, t0:t1], pp[0:4, :], AF.Identity, bias=bgt[:, ty:ty + 1])
                mout = alloc((4, nseq, 2), F32)
                onesT = alloc((4, L), F32)
                P.memset("pool", onesT, 1.0)
                e_ = alloc((4, T), F32)
                Bp = alloc((4, T), F32)
                a_ = alloc((4, T), F32)
                A_ = alloc((4, T), F32)
                nM_ = alloc((4, T), F32)
                nD_ = alloc((4, T), F32)
                for d in range(2):
                    li, fp = Gt[2 * d], Gt[2 * d + 1]
                    P.act(e_, fp, AF.Exp, scale=-1.0)
                    P.act(e_, e_, AF.Ln, bias=1.0)
                    for s in range(nseq):
                        sl = slice(s * L, (s + 1) * L)
                        def dirv(b_):
                            ap = b_.ap[:, sl]
                            return b_.v(ap if d == 0 else ap[:, ::-1])
                        P.scan(dirv(Bp), onesT, dirv(e_), 0.0, ALU.mult, ALU.add)
                    P.tt("dve", a_, li, Bp, ALU.add)
                    m0c = m0t[:, l * 2 + d:l * 2 + d + 1] if ctx else 0.0
                    for s in range(nseq):
                        sl = slice(s * L, (s + 1) * L)
                        def dirv(b_):
                            ap = b_.ap[:, sl]
                            return b_.v(ap if d == 0 else ap[:, ::-1])
                        P.scan(dirv(A_), dirv(a_), dirv(a_), m0c, ALU.max, ALU.max)
                    P.tt("dve", nM_, Bp, A_, ALU.subtract)
                    for s in range(nseq):
                        for c in range(nch_seq):
                            col0 = s * L + c * 128
                            first = (c == 0) if d == 0 else (c == nch_seq - 1)
                            if first:
                                prev = m0c
                            else:
                                pc_ = col0 - 1 if d == 0 else col0 + 128
                                prev = A_[:, pc_:pc_ + 1]
                            P.ts("dve", nD_[:, col0:col0 + 128], A_[:, col0:col0 + 128], prev, ALU.subtract)
                    if not ctx:
                        for s in range(nseq):
                            col = (s + 1) * L - 1 if d == 0 else s * L
                            P.ts("dve", mout[:, s, d:d + 1], nM_[:, col:col + 1], -1.0, ALU.mult)
                    for kind, tl in enumerate((A_, nD_, nM_, a_)):
                        P.dma("sp", gsc.v(gsc.ap[d, :, kind, 0:T]), tl, reads=[tl], writes=[gsc])
                if not ctx:
                    for s in range(nseq):
                        P.dma("sp", O["n_m"][s, l].rearrange("d h -> h d"), mout[:, s, :], reads=[mout])
                areset()
                mask = alloc((128, 2, 128), F32)
                P.dma("sp", mask, I["k_mask"], writes=[mask])
                cwt = alloc((128, 4, 3), F32)
                cbt = alloc((128, 4), F32)
                gml = alloc((128, 4), F32)
                for j_ in range(3):
                    P.dma("sp", cwt[:, :, j_], I["w_ml_conv"][l, j_].rearrange("(c p) -> p c", p=128), writes=[cwt])
                P.dma("sp", cbt, I["b_ml_conv"][l].rearrange("(c p) -> p c", p=128), writes=[cbt])
                P.dma("sp", gml, I["g_ml_out"][l].rearrange("(c p) -> p c", p=128), writes=[gml])
                oT = alloc((128, 4, T), BF16)
                wh2 = [alloc((128, 8, 384), BF16) for _ in range(1)]
                wqk2 = [alloc((128, 2, 128), BF16) for _ in range(2)]
                ug = alloc((128, nseq, L + 2), F32)
                tcv = alloc((128, nseq, L), F32)
                ucT = alloc((128, T), BF16)
                qT = alloc((128, T), BF16)
                kTm = alloc((128, T), BF16)
                ktok = alloc((128, NCH, 128), BF16)
                vau = alloc((128, NCH, 129), BF16)
                P.memset("pool", vau[:, :, 128:129], 1.0)
                sigo = alloc((128, T), BF16)
                hf = tcv.v(tcv.ap.rearrange("p a b -> p (a b)").rearrange("p (c e) -> p c e", e=128))
                kwb = [alloc((128, 4, 128), BF16) for _ in range(2)]
                Ugb = [alloc((128, 4, 129), F32) for _ in range(2)]
                CMb = [alloc((128, 5 if ctx else 6, 129), F32) for _ in range(3)]
                cmbb = [alloc((128, 4 if ctx else 5, 129), BF16) for _ in range(2)]
                smb = [alloc((128, 24), F32) for _ in range(2)]
                ugf = ug.ap.rearrange("p a b -> p (a b)")
                hsb = [ug.v(ugf[:, k_ * 512:(k_ + 1) * 512].rearrange("p (c e) -> p c e", e=128)) for k_ in range(2)]
                hnb = [alloc((128, 4, 128), BF16) for _ in range(2)]
                cmi = [0]
                ucf = ucT.ap.bitcast(F32)
                aw_ = min(512, T // 4)
                argc = [ucT.v(ucf[:, k_ * aw_:(k_ + 1) * aw_]) for k_ in range(2)] if ctx else [alloc((128, 512), F32) for _ in range(2)]
                wT = [alloc((128, 512), F32) for _ in range(2)]
                sT = [alloc((128, 512), BF16) for _ in range(2)]
                ie = [alloc((128, 512), F32) for _ in range(2)]
                qi = [alloc((128, 512), BF16) for _ in range(2)]
                junk = alloc((128, 128), F32)
                gls = [alloc((128, 2, 4), F32) for _ in range(2)]
                gi_c = [0]

                def head_pro(h):
                    wh, wqk = wh2[0], wqk2[h % 2]
                    wload(("wh", l, h), wh, I["wx"][l][:, 1472 + h * 384:1472 + (h + 1) * 384].rearrange("(k p) c -> p k c", p=128), ctx)
                    wload_multi([(("wq", l, h), wqk[:, 0, :], I["w_ml_q"][l, h]), (("wk", l, h), wqk[:, 1, :], I["w_ml_k"][l, h])], ctx)
                    P.memset("pool", ug[:, :, 0:1], 0.0)
                    P.memset("pool", ug[:, :, L + 1:L + 2], 0.0)
                    for tb in range(NB):
                        t0, t1 = tb * 512, (tb + 1) * 512
                        pu, po = psum(), psum()
                        for kc in range(8):
                            P.mm(pu, wh[:, kc, 0:128], Y(kc, t0, t1), start=(kc == 0), stop=(kc == 7))
                        for kc in range(8):
                            P.mm(po, wh[:, kc, 256:384], Y(kc, t0, t1), start=(kc == 0), stop=(kc == 7))
                        if L >= 512:
                            s0, o0 = t0 // L, t0 % L
                            P.copy("act", ug[:, s0, 1 + o0:1 + o0 + 512], pu)
                        else:
                            nsq = 512 // L
                            s0 = t0 // L
                            P.copy("act", ug[:, s0:s0 + nsq, 1:L + 1], pu.v(pu.ap.rearrange("p (a b) -> p a b", a=nsq)))
                        P.act(sigo[:, t0:t1], po, AF.Sigmoid)
                    P.ts("dve", tcv, ug[:, :, 1:L + 1], cwt[:, h, 1:2], ALU.mult, cbt[:, h:h + 1], ALU.add)
                    P.stt("dve", tcv, ug[:, :, 0:L], cwt[:, h, 0:1], tcv, ALU.mult, ALU.add)
                    P.stt("dve", tcv, ug[:, :, 2:L + 2], cwt[:, h, 2:3], tcv, ALU.mult, ALU.add)
                    P.act(ucT, tcv.v(tcv.ap.rearrange("p a b -> p (a b)")), AF.Silu)
                    for tb in range(NB):
                        t0, t1 = tb * 512, (tb + 1) * 512
                        pq, pk = psum(), psum()
                        P.mm(pq, wqk[:, 0, :], ucT[:, t0:t1])
                        P.mm(pk, wqk[:, 1, :], ucT[:, t0:t1])
                        P.copy("act", qT[:, t0:t1], pq)
                        P.act(kTm[:, t0:t1], pk, AF.Identity, scale=float(128 ** -0.5))
                        pkt = psum()
                        for c4 in range(4):
                            c = tb * 4 + c4
                            P.mm(pkt[:, c4 * 128:(c4 + 1) * 128], ucT[:, c * 128:(c + 1) * 128], wqk[:, 1, :])
                        P.act(ktok[:, tb * 4:tb * 4 + 4, :], pkt.v(pkt.ap.rearrange("p (a b) -> p a b", a=4)), AF.Identity, scale=float(128 ** -0.5))
                        pv = psum()
                        for c4 in range(4):
                            c = tb * 4 + c4
                            for kc in range(8):
                                P.mm(pv[:, c4 * 128:(c4 + 1) * 128], Y(kc, c * 128, (c + 1) * 128), wh[:, kc, 128:256], start=(kc == 0), stop=(kc == 7))
                        P.copy("dve", vau[:, tb * 4:tb * 4 + 4, 0:128], pv.v(pv.ap.rearrange("p (a b) -> p a b", a=4)))

                ngr = (nch_seq + 3) // 4
                units = []
                PAIR = (not ctx) and nch_seq == 2 and nseq % 2 == 0
                for h in range(4):
                    for d in range(2):
                        if PAIR:
                            for sp_ in range(nseq // 2):
                                units.append((h, d, sp_, 0, True, True, d == 0 and sp_ == 0))
                            continue
                        for s in range(nseq):
                            gorder = list(range(ngr)) if d == 0 else list(range(ngr - 1, -1, -1))
                            for gi2, gq_ in enumerate(gorder):
                                units.append((h, d, s, gq_, gi2 == 0, gi2 == ngr - 1, d == 0 and s == 0 and gi2 == 0))
                uctx = {}
                uctx1 = {}
                uctx2 = {}

                def unit_A(u):
                    h, d, s, gq_, first_sd, last_sd, first_h = u
                    if first_h:
                        head_pro(h)
                    gi_ = gi_c[0]
                    cg0 = gq_ * 4
                    ng = min(4, nch_seq - cg0)
                    col0 = s * L + cg0 * 128
                    if PAIR:
                        ng = 4
                        col0 = s * 512
                    W = ng * 128
                    ac, w_, s_, ie_, qi__ = argc[gi_ % 2], wT[gi_ % 2], sT[gi_ % 2], ie[gi_ % 2], qi[gi_ % 2]
                    gl = gls[gi_ % 2]
                    gi_c[0] += 1
                    gi_ = gi_c[0]
                    P.dma("sp", ac[:, 0:W], gsc.v(gsc.ap[d, h, 0, col0:col0 + W].partition_broadcast(128)), reads=[gsc], writes=[ac])
                    P.dma("sp", ie_[:, 0:W], gsc.v(gsc.ap[d, h, 1, col0:col0 + W].partition_broadcast(128)), reads=[gsc], writes=[ie_])
                    P.dma("sp", gl[:, 0, 0:ng], gsc.v(gsc.ap[d, h, 3, col0:col0 + W].rearrange("(i p) -> p i", p=128)), reads=[gsc], writes=[gl])
                    P.dma("sp", gl[:, 1, 0:ng], gsc.v(gsc.ap[d, h, 2, col0:col0 + W].rearrange("(i p) -> p i", p=128)), reads=[gsc], writes=[gl])
                    for i in range(ng):
                        P.stt("dve", w_[:, i * 128:(i + 1) * 128], ac[:, i * 128:(i + 1) * 128], gl[:, 0, i:i + 1], mask[:, d, :], ALU.subtract, ALU.max)
                    P.act(w_[:, 0:W], w_[:, 0:W], AF.Exp, scale=-1.0)
                    pst = psum()
                    for i in range(ng):
                        cc0 = col0 + i * 128
                        P.mm(pst[:, i * 128:(i + 1) * 128], kTm[:, cc0:cc0 + 128], qT[:, cc0:cc0 + 128])
                    P.tt("dve", s_[:, 0:W], pst[:, 0:W], w_[:, 0:W], ALU.mult)
                    P.act(ie_[:, 0:W], ie_[:, 0:W], AF.Exp, scale=-1.0)
                    P.tt("pool", qi__[:, 0:W], qT[:, col0:col0 + W], ie_[:, 0:W], ALU.mult)
                    corder = list(range(ng)) if d == 0 else list(range(ng - 1, -1, -1))
                    ecs = [(i * 128 + 127) if d == 0 else (i * 128) for i in range(ng)]
                    cs_ = [s * nch_seq + cg0 + i for i in range(ng)]
                    if PAIR:
                        corder = [0, 1, 2, 3] if d == 0 else [1, 0, 3, 2]
                        cs_ = [s * 4 + i for i in range(4)]
                    n3 = min(ng, 3)
                    kwg, Ug, cmbg, sm_ = kwb[gi_ % 2], Ugb[gi_ % 2], cmbb[gi_ % 2], smb[gi_ % 2]
                    hs4, hn4 = hsb[gi_ % 2], hnb[gi_ % 2]
                    P.act(sm_[:, 0:ng], gl[:, 1, 0:ng], AF.Exp)
                    uctx1[u] = (ng, w_, ecs, cs_, n3, kwg, Ug)
                    uctx[u] = (cg0, ng, col0, W, gl, w_, s_, ie_, qi__, corder, ecs, cs_, n3, Ug, cmbg, sm_, hs4, hn4)

                def unit_A2(u):
                    (ng, w_, ecs, cs_, n3, kwg, Ug) = uctx1.pop(u)
                    e0_ = ecs[0]
                    wkb = w_.v(w_.ap[:, 0:ng * 128].rearrange("p (a b) -> p a b", b=128)[:, :, e0_:e0_ + 1].broadcast_to([128, ng, 128]))
                    P.tt("dve", kwg[:, 0:ng, :], ktok[:, cs_[0]:cs_[0] + ng, :], wkb, ALU.mult)
                    pUa = psum()
                    pUb = psum() if ng == 4 else None
                    for i in range(ng):
                        dst_ = pUa[:, i * 129:(i + 1) * 129] if i < 3 else pUb[:, 0:129]
                        P.mm(dst_, kwg[:, i, :], vau[:, cs_[i], :])
                    P.copy("act", Ug[:, 0:n3, :], pUa.v(pUa.ap[:, 0:n3 * 129].rearrange("p (a b) -> p a b", a=n3)))
                    if ng == 4:
                        P.copy("act", Ug[:, 3, :], pUb[:, 0:129])

                def unit_B(u):
                    h, d, s, gq_, first_sd, last_sd, first_h = u
                    (cg0, ng, col0, W, gl, w_, s_, ie_, qi__, corder, ecs, cs_, n3, Ug, cmbg, sm_, hs4, hn4) = uctx.pop(u)
                    if first_sd:
                        cm0 = CMb[cmi[0] % 3]
                        if ctx:
                            P.dma("sp", cm0[:, 0, 0:128], I["C0"][l, d, h], writes=[cm0])
                            P.dma("sp", cm0[:, 0, 128:129], I["n0"][l, d, h].rearrange("(p o) -> p o", o=1), writes=[cm0])
                        else:
                            P.memset("pool", cm0[:, 0, :], 0.0)
                            if PAIR:
                                P.memset("pool", cm0[:, 3, :], 0.0)
                    CMg = CMb[cmi[0] % 3]
                    CMn = CMb[(cmi[0] + 1) % 3]
                    cmi[0] += 1
                    if PAIR:
                        slot = [0, 1, 3, 4]
                        for k, i in enumerate(corder):
                            P.stt("dve", CMg[:, slot[k] + 1, :], CMg[:, slot[k], :], ie_[:, ecs[i]:ecs[i] + 1], Ug[:, i, :], ALU.mult, ALU.add)
                        P.copy("act", cmbg[:, 0:5, :], CMg[:, 0:5, :])
                    else:
                        slot = list(range(ng))
                        for k, i in enumerate(corder):
                            out_ = CMg[:, k + 1, :] if k < ng - 1 else CMn[:, 0, :]
                            P.stt("dve", out_, CMg[:, k, :], ie_[:, ecs[i]:ecs[i] + 1], Ug[:, i, :], ALU.mult, ALU.add)
                        P.copy("act", cmbg[:, 0:ng, :], CMg[:, 0:ng, :])
                    pnA = psum()
                    pnB = psum()
                    for k, i in enumerate(corder):
                        dst_ = pnA[:, i * 129:(i + 1) * 129] if i < 3 else pnB[:, 0:129]
                        P.mm(dst_, s_[:, i * 128:(i + 1) * 128], vau[:, cs_[i], :], start=True, stop=False)
                        P.mm(dst_, qi__[:, i * 128:(i + 1) * 128], cmbg[:, slot[k], :], start=False, stop=True)
                    uctx2[u] = (cg0, ng, col0, W, cs_, n3, sm_, hs4, hn4, pnA, pnB, CMg, gl)

                def unit_B2(u):
                    h, d, s, gq_, first_sd, last_sd, first_h = u
                    (cg0, ng, col0, W, cs_, n3, sm_, hs4, hn4, pnA, pnB, CMg_, gl_) = uctx2.pop(u)
                    denA = pnA.v(pnA.ap[:, 0:n3 * 129].rearrange("p (a b) -> p a b", b=129)[:, :, 128])
                    P.ts("dve", sm_[:, 4:4 + n3], denA, -1.0, ALU.mult)
                    P.tt("dve", sm_[:, 4:4 + n3], sm_[:, 4:4 + n3], denA, ALU.max)
                    if ng == 4:
                        P.ts("dve", sm_[:, 7:8], pnB[:, 128:129], -1.0, ALU.mult)
                        P.tt("dve", sm_[:, 7:8], sm_[:, 7:8], pnB[:, 128:129], ALU.max)
                    P.tt("dve", sm_[:, 4:4 + ng], sm_[:, 4:4 + ng], sm_[:, 0:ng], ALU.max)
                    P.recip(sm_[:, 8:8 + ng], sm_[:, 4:4 + ng])
                    if d == 0:
                        rcb = sm_.v(sm_.ap[:, 8:8 + n3].unsqueeze(2).broadcast_to([128, n3, 128]))
                        P.tt("dve", hf[:, cs_[0]:cs_[0] + n3, :], pnA.v(pnA.ap[:, 0:n3 * 129].rearrange("p (a b) -> p a b", b=129)[:, :, 0:128]), rcb, ALU.mult)
                        if ng == 4:
                            P.ts("dve", hf[:, cs_[3], :], pnB[:, 0:128], sm_[:, 11:12], ALU.mult)
                    for i in range(ng):
                        src_ = pnA[:, i * 129:i * 129 + 128] if i < 3 else pnB[:, 0:128]
                        if d == 0:
                            pass
                        else:
                            P.stt("dve", hs4[:, i, :], src_, sm_[:, 8 + i:9 + i], hf[:, cs_[i], :], ALU.mult, ALU.add)
                            P.act(junk, hs4[:, i, :], AF.Square, accum=sm_[:, 12 + i:13 + i])
                    if d == 1:
                        P.act(sm_[:, 16:16 + ng], sm_[:, 12:12 + ng], AF.Ln, bias=EPS, scale=1.0 / 128)
                        P.act(sm_[:, 20:20 + ng], sm_[:, 16:16 + ng], AF.Exp, scale=-0.5)

                        def b3(h=h, ng=ng, hn4=hn4, hs4=hs4, sm_=sm_, col0=col0, W=W):
                            for i in range(ng):
                                P.ts("dve", hn4[:, i, :], hs4[:, i, :], sm_[:, 20 + i:21 + i], ALU.mult)
                            ptr = psum()
                            ptb = pbf(ptr)
                            for i in range(ng):
                                P.tr(ptb[:, i * 128:(i + 1) * 128], hn4[:, i, :], identb)
                            P.stt("dve", oT[:, h, col0:col0 + W], ptb[:, 0:W], gml[:, h:h + 1], sigo[:, col0:col0 + W], ALU.mult, ALU.mult)
                        pend3.append(b3)
                    if last_sd:
                        cmfin = CMb[cmi[0] % 3]
                        if PAIR:
                            for sq_, sl_ in ((2 * s, 2), (2 * s + 1, 5)):
                                P.dma("sp", O["n_C"][sq_, l, d, h], CMg_[:, sl_, 0:128], reads=[CMg_])
                                P.dma("sp", O["n_n"][sq_, l, d, h].rearrange("(p o) -> p o", o=1), CMg_[:, sl_, 128:129], reads=[CMg_])
                        elif not ctx:
                            P.dma("sp", O["n_C"][s, l, d, h], cmfin[:, 0, 0:128], reads=[cmfin])
                            P.dma("sp", O["n_n"][s, l, d, h].rearrange("(p o) -> p o", o=1), cmfin[:, 0, 128:129], reads=[cmfin])

                pend3 = []
                unit_A(units[0])
                unit_A2(units[0])
                for ui, u in enumerate(units):
                    nxt = units[ui + 1] if ui + 1 < len(units) else None
                    pipel = nxt is not None and not nxt[6]
                    unit_B(u)
                    if pipel:
                        unit_A(nxt)
                        unit_A2(nxt)
                    while pend3:
                        pend3.pop(0)()
                    unit_B2(u)
                    if nxt is None or nxt[6]:
                        while pend3:
                            pend3.pop(0)()
                    if nxt is not None and nxt[6]:
                        unit_A(nxt)
                        unit_A2(nxt)
                outproj(l, oT, 256, 4)

            return mlstm_, ffn_

        if do_p:
            mlstm, ffn = make_stage_fns(0, 4, 256, False, False)
            run_group("P", 0, I["xp"], O["yp"], 4, 256, False, False)
        if do_s:
            mlstm, ffn = make_stage_fns(1, 1, 2048, True, True)
            run_group("S", 1, I["xs"], O["ys"], 1, 2048, True, True)
        P.barrier()
        P.emit()
    return nc, P


_CACHE = {}


def kernel(x_prompt, x_sample, cache_mla_ckv, cache_mla_krope, cache_gqa_k, cache_gqa_v,
           state_mlstm_C, state_mlstm_n, state_mlstm_m, c, c_ctx,
           w_ada, b_ada, g_norm1, g_norm2, w_in, g_mla_q, w_mla_uq, g_mla_kv, w_mla_ukv,
           w_ml_conv, b_ml_conv, w_ml_q, w_ml_k, b_ml_gates, g_ml_out, g_gqa_q, g_gqa_k,
           w_out, w_ff_up, w_ff_conv, b_ff_conv, w_ff_down, g_final, _do_p=True, _do_s=True, _cores=8):
    f = lambda a: np.ascontiguousarray(np.asarray(a, dtype=np.float32))
    wx, uqx, ukvx, gq, bg = _layout_weights(f(w_in), f(w_mla_uq), f(w_mla_ukv), f(g_gqa_q), f(g_gqa_k), f(b_ml_gates))
    consts = _consts()
    shared = {
        "w_ada": f(w_ada), "b_ada": f(b_ada), "g_norm1": f(g_norm1), "g_norm2": f(g_norm2), "wx": wx,
        "g_mla_q": f(g_mla_q), "uqx": uqx, "g_mla_kv": f(g_mla_kv), "ukvx": ukvx,
        "w_ml_conv": f(w_ml_conv), "b_ml_conv": f(b_ml_conv), "w_ml_q": f(w_ml_q), "w_ml_k": f(w_ml_k),
        "bg": bg, "g_ml_out": f(g_ml_out), "gq": gq, "g_gqa_k": f(g_gqa_k),
        "w_out": f(w_out), "w_ff_up": f(w_ff_up), "w_ff_conv": f(w_ff_conv), "b_ff_conv": f(b_ff_conv),
        "w_ff_down": f(w_ff_down), "g_final": f(g_final),
    }
    shared.update(consts)
    xp, xs = f(x_prompt), f(x_sample)
    in_maps = []
    for i in range(_cores):
        m = dict(shared)
        m["xp"] = xp[4 * i:4 * i + 4].reshape(1024, D)
        m["xs"] = xs[i]
        m["c_ckv"] = f(cache_mla_ckv)[i]
        m["c_kr"] = f(cache_mla_krope)[i]
        m["c_gk"] = f(cache_gqa_k)[i].reshape(DEPTH, 256, 128)
        m["c_gv"] = f(cache_gqa_v)[i].reshape(DEPTH, 256, 128)
        m["C0"] = f(state_mlstm_C)[i]
        m["n0"] = f(state_mlstm_n)[i]
        m["m0"] = f(state_mlstm_m)[i]
        m["cvec"] = np.stack([f(c_ctx), f(c)[i]], 0)
        in_maps.append(m)
    key = (_do_p, _do_s)
    if key not in _CACHE:
        _CACHE[key] = build_program(_do_p, _do_s)[0]
    nc = _CACHE[key]
    res = run_bass_kernel_spmd(nc, in_maps, core_ids=list(range(_cores)))
    R = res.results
    cat = lambda k: np.concatenate([r[k] for r in R], 0)
    y_prompt = cat("yp").reshape(-1, 256, D)
    y_sample = np.stack([r["ys"] for r in R], 0)
    n_ckv = cat("n_ckv")
    n_kr = cat("n_kr")
    n_k = cat("n_k").reshape(-1, DEPTH, 256, 2, 64)
    n_v = cat("n_v").reshape(-1, DEPTH, 256, 2, 64)
    return (y_prompt, y_sample, n_ckv, n_kr, n_k, n_v, cat("n_C"), cat("n_n"), cat("n_m"))
```
